# Optimizing a Trainium2 kernel written in Bass

```python
import jax, jax.numpy as jnp
from jax import lax
import numpy as np

D_MODEL = 1024
BATCH = 8
SEQ = 2048
DEPTH = 2

CHUNK = 64
HEAD_DIM = 128
MIX_WIDTH = D_MODEL
RET_WIDTH = MIX_WIDTH // 2
SB_WIDTH = MIX_WIDTH - RET_WIDTH
N_RET_HEADS = RET_WIDTH // HEAD_DIM
N_SB_HEADS = SB_WIDTH // HEAD_DIM
D_FF = ((8 * D_MODEL // 3 + 255) // 256) * 256
SB_BLOCK = 128
ROPE_BASE = 10000.0
EPS = 1e-6
IN_WIDTHS = (RET_WIDTH, RET_WIDTH, RET_WIDTH, RET_WIDTH, SB_WIDTH, SB_WIDTH, SB_WIDTH)
IN_WIDTH = sum(IN_WIDTHS)
IN_SPLITS = tuple(int(c) for c in np.cumsum(IN_WIDTHS)[:-1])

kernel_name = "hymba_retention_stickbreaking_trunk"


def rms_norm(x, g):
    xf = x.astype(jnp.float32)
    y = xf * lax.rsqrt(jnp.mean(xf * xf, axis=-1, keepdims=True) + EPS)
    return (y * g.astype(jnp.float32)).astype(x.dtype)


def to_heads(t, n_heads):
    b, s, _ = t.shape
    return t.reshape(b, s, n_heads, HEAD_DIM).transpose(0, 2, 1, 3)


def from_heads(t):
    b, h, s, d = t.shape
    return t.transpose(0, 2, 1, 3).reshape(b, s, h * d)


def head_group_norm(x, g):
    h, d = x.shape[1], x.shape[3]
    xf = x.astype(jnp.float32)
    mu = jnp.mean(xf, axis=-1, keepdims=True)
    var = jnp.mean(jnp.square(xf - mu), axis=-1, keepdims=True)
    y = (xf - mu) * lax.rsqrt(var + EPS) * g.astype(jnp.float32).reshape(h, 1, d)
    return y.astype(x.dtype)


def head_rms_norm(x, g):
    h, d = x.shape[1], x.shape[3]
    xf = x.astype(jnp.float32)
    y = xf * lax.rsqrt(jnp.mean(xf * xf, axis=-1, keepdims=True) + EPS)
    return (y * g.astype(jnp.float32).reshape(h, 1, d)).astype(x.dtype)


def apply_rotary(x):
    s, d = x.shape[2], x.shape[3]
    inv_freq = 1.0 / (ROPE_BASE ** (jnp.arange(0, d, 2, dtype=jnp.float32) / d))
    ang = jnp.arange(s, dtype=jnp.float32)[:, None] * inv_freq[None, :]
    cos = jnp.cos(ang).astype(x.dtype)
    sin = jnp.sin(ang).astype(x.dtype)
    x1, x2 = x[..., : d // 2], x[..., d // 2:]
    return jnp.concatenate([x1 * cos - x2 * sin, x1 * sin + x2 * cos], axis=-1)


def chunk_retention(q, k, v):
    b, h, s, d = q.shape
    c = CHUNK
    n = s // c
    dt = q.dtype
    log_g = jnp.log1p(-jnp.exp2(-5.0 - jnp.arange(h, dtype=jnp.float32)))
    i = jnp.arange(c, dtype=jnp.float32)
    intra_decay = jnp.exp(log_g[:, None, None] * jnp.abs(i[:, None] - i[None, :])).astype(dt)
    q_decay = jnp.exp(log_g[:, None] * (i + 1.0)).astype(dt)[..., None]
    k_decay = jnp.exp(log_g[:, None] * (c - 1.0 - i)).astype(dt)[..., None]
    chunk_decay = jnp.exp(log_g * c).astype(dt)[None, :, None, None]
    k = k * (d ** -0.5)
    qc = q.reshape(b, h, n, c, d)
    kc = k.reshape(b, h, n, c, d)
    vc = v.reshape(b, h, n, c, v.shape[-1])
    scores = jnp.einsum('bhnid,bhnjd->bhnij', qc, kc) * intra_decay[:, None]
    intra = jnp.einsum('bhnij,bhnje->bhnie', scores, vc)

    def step(state, inp):
        q_n, k_n, v_n = inp
        cross = jnp.einsum('bhid,bhde->bhie', q_n * q_decay, state)
        state = state * chunk_decay + jnp.einsum('bhjd,bhje->bhde', k_n * k_decay, v_n)
        return state, cross

    init = jnp.zeros((b, h, d, v.shape[-1]), dtype=intra.dtype)
    xs = (jnp.moveaxis(qc, 2, 0), jnp.moveaxis(kc, 2, 0), jnp.moveaxis(vc, 2, 0))
    _, cross = lax.scan(step, init, xs)
    out = intra + jnp.moveaxis(cross, 0, 2)
    return out.reshape(b, h, s, v.shape[-1])


def stick_breaking(q, k, v):
    s, d = q.shape[2], q.shape[3]
    scale = d ** -0.5
    outs = []
    for start in range(0, s, SB_BLOCK):
        end = start + SB_BLOCK
        qb = q[:, :, start:end]
        kb = k[:, :, :end]
        vb = v[:, :, :end]
        z = jnp.einsum('bhtd,bhsd->bhts', qb, kb).astype(jnp.float32) * scale
        t_pos = jnp.arange(start, end)[:, None]
        s_pos = jnp.arange(end)[None, :]
        valid = s_pos < t_pos
        log_keep = jnp.where(valid, jax.nn.log_sigmoid(-z), 0.0)
        later = lax.cumsum(log_keep, axis=3, reverse=True) - log_keep
        w = jnp.where(valid, jnp.exp(jax.nn.log_sigmoid(z) + later), 0.0)
        outs.append(jnp.einsum('bhts,bhse->bhte', w.astype(v.dtype), vb))
    return jnp.concatenate(outs, axis=2)


def setup_inputs(seed: int = 0) -> dict:
    key = jax.random.key(seed)
    ks = jax.random.split(key, 12)
    f32 = jnp.float32

    def gain(k, shape):
        return (1.0 + 0.02 * jax.random.normal(k, shape, f32)).astype(f32)

    return {
        "x": jax.random.normal(ks[0], (BATCH, SEQ, D_MODEL), f32),
        "norm1_g": gain(ks[1], (DEPTH, D_MODEL)),
        "w_in": jax.random.normal(ks[2], (DEPTH, D_MODEL, IN_WIDTH), f32) * D_MODEL ** -0.5,
        "ret_norm_g": gain(ks[3], (DEPTH, RET_WIDTH)),
        "sb_norm_g": gain(ks[4], (DEPTH, SB_WIDTH)),
        "w_out": jax.random.normal(ks[5], (DEPTH, MIX_WIDTH, D_MODEL), f32) * MIX_WIDTH ** -0.5,
        "norm2_g": gain(ks[6], (DEPTH, D_MODEL)),
        "w_gate": jax.random.normal(ks[7], (DEPTH, D_MODEL, D_FF), f32) * D_MODEL ** -0.5,
        "w_up": jax.random.normal(ks[8], (DEPTH, D_MODEL, D_FF), f32) * D_MODEL ** -0.5,
        "w_down": jax.random.normal(ks[9], (DEPTH, D_FF, D_MODEL), f32) * D_FF ** -0.5,
        "final_g": gain(ks[10], (D_MODEL,)),
    }


def reference(x, norm1_g, w_in, ret_norm_g, sb_norm_g, w_out, norm2_g, w_gate, w_up, w_down, final_g):
    for l in range(DEPTH):
        h = rms_norm(x, norm1_g[l])
        proj = h @ w_in[l]
        rq, rk, rv, rg, sq, sk, sv = jnp.split(proj, IN_SPLITS, axis=-1)
        ret = chunk_retention(apply_rotary(to_heads(rq, N_RET_HEADS)),
                              apply_rotary(to_heads(rk, N_RET_HEADS)),
                              to_heads(rv, N_RET_HEADS))
        ret = from_heads(head_group_norm(ret, ret_norm_g[l])) * jax.nn.silu(rg)
        sb = stick_breaking(to_heads(sq, N_SB_HEADS), to_heads(sk, N_SB_HEADS), to_heads(sv, N_SB_HEADS))
        sb = from_heads(head_rms_norm(sb, sb_norm_g[l]))
        x = x + jnp.concatenate([ret, sb], axis=-1) @ w_out[l]
        h = rms_norm(x, norm2_g[l])
        x = x + (jax.nn.silu(h @ w_gate[l]) * (h @ w_up[l])) @ w_down[l]
    return rms_norm(x, final_g)
```

```python
import contextlib
import numpy as np
import ml_dtypes
import concourse.bass as bass
import concourse.mybir as mybir
from concourse.ap import AP
from concourse.bass_utils import run_bass_kernel_spmd

F32 = mybir.dt.float32
BF16 = mybir.dt.bfloat16
ALU = mybir.AluOpType
AF = mybir.ActivationFunctionType

D = 1024
S = 2048
DEPTH = 2
HD = 128
NH = 4
DFF = 2816
NFC = DFF // 128
NB = S // 128
NT = 4
TW = 512
INW = 3584
EPS = 1e-6
NCORES = 8
RING = 12
ARENA_BYTES = 212480
FG = 11
SCALE = float(HD ** -0.5)


class T:
    __slots__ = ("name", "w", "rc", "rd")

    def __init__(self, name=""):
        self.name = name
        self.w = None
        self.rc = {}
        self.rd = []


class Op:
    __slots__ = ("eng", "fn", "deps", "idx", "signal", "cnt", "dma", "dsem", "dtarget", "dprev")


class Prog:
    ENG = ("pe", "act", "dve", "pool", "sp")

    def __init__(self, nc, ndsem=12):
        self.nc = nc
        self.ops = {e: [] for e in self.ENG}
        self.ndsem = ndsem
        self.pending_bar = {}
        self.dma_since_bar = []

    def barrier(self):
        deps = []
        for e in self.ENG:
            for o in reversed(self.ops[e]):
                if not o.dma:
                    deps.append(o)
                    break
        deps.extend(self.dma_since_bar)
        self.dma_since_bar = []
        for e in self.ENG:
            self.pending_bar[e] = list(deps) + self.pending_bar.get(e, [])

    def op(self, eng, fn, reads=(), writes=(), dma=0, nobar=False):
        o = Op()
        o.eng = eng
        o.fn = fn
        o.dma = dma
        o.signal = False
        o.cnt = 0
        deps = set()
        for t in reads:
            if t.w is not None:
                deps.add(t.w)
        for t in writes:
            if t.w is not None:
                deps.add(t.w)
            deps.update(t.rc.values())
            deps.update(t.rd)
        for t in reads:
            if dma:
                t.rd.append(o)
            else:
                t.rc[eng] = o
        for t in writes:
            t.w = o
            t.rc = {}
            t.rd = []
        deps.discard(o)
        if eng == "pe" and not dma:
            deps = {d for d in deps if d.dma or d.eng != "pe"}
        if eng in self.pending_bar:
            deps.update(self.pending_bar.pop(eng))
        best = {}
        out = []
        for d in deps:
            if d.dma:
                out.append(d)
            else:
                b = best.get(d.eng)
                if b is None or d.idx > b.idx:
                    best[d.eng] = d
        out.extend(best.values())
        o.deps = out
        o.idx = len(self.ops[eng])
        self.ops[eng].append(o)
        if dma and not nobar:
            self.dma_since_bar.append(o)
        return o

    def emit(self):
        nc = self.nc
        for e in self.ENG:
            for o in self.ops[e]:
                for d in o.deps:
                    d.signal = True
        for e in self.ENG:
            c = 0
            for o in self.ops[e]:
                if not o.dma and o.signal:
                    c += 1
                o.cnt = c
        with contextlib.ExitStack() as st:
            esem = {e: st.enter_context(nc.semaphore("es_" + e)) for e in self.ENG}
            dsem = {e: [st.enter_context(nc.semaphore("ds_%s_%d" % (e, i))) for i in range(self.ndsem)]
                    for e in ("sp", "pool", "act")}
            for e in self.ENG:
                k = 0
                targets = [0] * self.ndsem
                for o in self.ops[e]:
                    if o.dma:
                        slot = k % self.ndsem
                        o.dsem = dsem[e][slot]
                        o.dprev = targets[slot]
                        targets[slot] += 16 * o.dma
                        o.dtarget = targets[slot]
                        k += 1
            block = st.enter_context(nc.Block())

            def run(eng_name):
                def body(e):
                    known = {}
                    for o in self.ops[eng_name]:
                        waits = {}
                        for d in o.deps:
                            if d.dma:
                                key, val = d.dsem, d.dtarget
                            else:
                                key, val = esem[d.eng], d.cnt
                            if waits.get(key, (None, 0))[1] < val:
                                waits[key] = (key, val)
                        if o.dma and o.dprev > 0:
                            if waits.get(o.dsem, (None, 0))[1] < o.dprev:
                                waits[o.dsem] = (o.dsem, o.dprev)
                        for key, val in waits.values():
                            if known.get(key, 0) < val:
                                e.wait_ge(key, val)
                                known[key] = val
                        if o.fn is None:
                            continue
                        rs_ = [getattr(e, nm)(*a, **kw) for (nm, a, kw) in o.fn]
                        if o.dma:
                            assert len(rs_) == o.dma
                            for ins in rs_:
                                ins.then_inc(o.dsem, 16)
                        elif o.signal:
                            rs_[-1].then_inc(esem[eng_name], 1)
                return body

            block.tensor(run("pe"))
            block.scalar(run("act"))
            block.vector(run("dve"))
            block.gpsimd(run("pool"))
            block.sync(run("sp"))


class Arena:
    def __init__(self, nc, nbytes):
        self.t = nc.alloc_sbuf_tensor("arena", [128, nbytes // 2], BF16)
        self.top = 0
        self.cap = nbytes
        self.peak = 0

    def alloc(self, free, dtype):
        n = int(np.prod(free))
        sz = n * (4 if dtype == F32 else 2)
        off = self.top
        self.top += (sz + 63) // 64 * 64
        self.peak = max(self.peak, self.top)
        assert self.top <= self.cap, "SBUF arena overflow %d > %d" % (self.top, self.cap)
        ap = self.t[:, off // 2: off // 2 + sz // 2]
        if dtype == F32:
            ap = ap.bitcast(F32)
        if len(free) == 2:
            ap = ap.rearrange("p (a b) -> p a b", a=free[0])
        elif len(free) == 3:
            ap = ap.rearrange("p (a b c) -> p a b c", a=free[0], b=free[1])
        return ap

    def mark(self):
        return self.top

    def release(self, m):
        self.top = m


def I(name, *a, **kw):
    return (name, a, kw)


def mkap(base, extra, dims):
    return AP(base.tensor, base.offset + extra, [list(base.ap[0])] + [list(d) for d in dims])


def make_consts():
    c = {}
    c["ident_bf"] = np.eye(128, dtype=np.float32).astype(ml_dtypes.bfloat16)
    c["ident_f"] = np.eye(128, dtype=np.float32)
    c["onesD"] = np.full((128, 128), 1.0 / D, dtype=np.float32).astype(ml_dtypes.bfloat16)
    c["onesH"] = np.full((128, 128), 1.0 / HD, dtype=np.float32).astype(ml_dtypes.bfloat16)
    inv_freq = (1.0 / (np.float32(10000.0) ** (np.arange(0, HD, 2, dtype=np.float32) / np.float32(HD)))).astype(np.float32)
    pos = np.arange(S, dtype=np.float32)
    ang = (pos[:, None] * inv_freq[None, :]).astype(np.float32)
    cos = np.cos(ang).astype(np.float32).reshape(NB, 128, 64).transpose(1, 0, 2)
    sin = np.sin(ang).astype(np.float32).reshape(NB, 128, 64).transpose(1, 0, 2)
    c["cos"] = np.ascontiguousarray(cos)
    c["sins"] = np.ascontiguousarray(np.stack([-sin, sin], axis=2))
    lg = np.log1p(-np.exp2(-5.0 - np.arange(NH, dtype=np.float64)))
    i = np.arange(128)
    ii, jj = i[None, :], i[:, None]
    ci, cj = ii // 64, jj // 64
    maskT = np.zeros((128, NH, 128), dtype=np.float64)
    for h in range(NH):
        same = np.exp(lg[h] * np.abs(ii - jj))
        prev = np.exp(lg[h] * (ii - jj).clip(min=0))
        maskT[:, h, :] = np.where(ci == cj, same, np.where(cj < ci, prev, 0.0))
    c["maskT"] = maskT.astype(np.float32)
    qd = np.exp(lg[:, None] * (i[None, :] + 1.0))
    c["qdec"] = np.ascontiguousarray(np.broadcast_to(qd[None], (128, NH, 128))).astype(np.float32)
    c["kdec"] = np.exp(lg[None, :] * (127.0 - i[:, None])).astype(np.float32)
    cd = [float(np.exp(lg[h] * 128.0)) for h in range(NH)]
    t = np.arange(128)
    c["mask01"] = (t[None, :] < t[:, None]).astype(np.float32)
    c["ones_col"] = np.ones((128, 1), dtype=np.float32)
    c["negmask"] = (np.float32(-30000.0) * (t[None, :] >= t[:, None]).astype(np.float32)).astype(ml_dtypes.bfloat16)
    c["mhalf"] = np.full((128, 1), -0.5, dtype=np.float32)
    c["cenM"] = (np.eye(128, dtype=np.float32) - np.float32(1.0 / HD)).astype(ml_dtypes.bfloat16)
    c["notmask"] = (t[None, :] >= t[:, None]).astype(np.float32)
    return c, cd


CONST_SPECS = [
    ("ident_bf", [128], BF16), ("ident_f", [128], F32), ("onesD", [128], BF16), ("onesH", [128], BF16),
    ("cos", [NB, 64], F32), ("sins", [NB, 2, 64], F32), ("maskT", [NH, 128], F32), ("qdec", [NH, 128], F32),
    ("kdec", [NH], F32), ("notmask", [128], F32), ("mask01", [128], F32), ("ones_col", [1], F32), ("cenM", [128], BF16), ("negmask", [128], BF16), ("mhalf", [1], F32),
]


def build_program(layers, first, final):
    nc = bass.Bass("TRN2", target_bir_lowering=False)
    _, CD = make_consts()

    def din(name, shape, dt=F32):
        return nc.dram_tensor(name, list(shape), dt, kind="ExternalInput").ap()

    x_d = din("x", [S, D])
    n1_d = din("norm1_gT", [128, DEPTH * 8])
    n2_d = din("norm2_gT", [128, DEPTH * 8])
    rg_d = din("ret_gT", [128, DEPTH * NH])
    sg_d = din("sb_gT", [128, DEPTH * NH])
    fg_d = din("final_gT", [128, 8])
    w_in_d = din("w_in", [DEPTH, D, INW])
    w_out_d = din("w_out", [DEPTH, D, D])
    w_gate_d = din("w_gate", [DEPTH, D, DFF])
    w_up_d = din("w_up", [DEPTH, D, DFF])
    w_down_d = din("w_down", [DEPTH, DFF, D])
    cdram = {}
    for name, free, dt in CONST_SPECS:
        cdram[name] = din("c_" + name, [128] + free, dt)
    y_d = nc.dram_tensor("y", [S, D], F32, kind="ExternalOutput").ap()

    A = Arena(nc, ARENA_BYTES)
    p = Prog(nc)
    ps = nc.alloc_psum_tensor("ps", [128, 8, 512], F32)
    Tps = [T("ps%d" % b) for b in range(8)]

    def psb(b):
        return ps[:, b, :]

    def psb16(b):
        return ps[:, b, :].bitcast(BF16)

    xT = A.alloc([8, S], F32)
    hT = A.alloc([8, S], BF16)
    Tx = [[T("x%d_%d" % (c, n)) for n in range(NT)] for c in range(8)]
    Th = [[T("h%d_%d" % (c, n)) for n in range(NT)] for c in range(8)]
    ring = [A.alloc([1024], BF16) for _ in range(RING)]
    Tring = [T("ring%d" % i) for i in range(RING)]
    C = {}
    TC = T("consts")
    for name, free, dt in CONST_SPECS:
        C[name] = A.alloc(free, dt)
    gvec = {}
    gsrc = (("n1", n1_d, DEPTH * 8), ("n2", n2_d, DEPTH * 8), ("rg", rg_d, DEPTH * NH), ("sg", sg_d, DEPTH * NH), ("fg", fg_d, 8))
    for name, src, n in gsrc:
        gvec[name] = A.alloc([n], F32)
    ins = [I("dma_start", out=C[name], in_=cdram[name]) for name, _, _ in CONST_SPECS]
    ins += [I("dma_start", out=gvec[name], in_=src) for name, src, _ in gsrc]
    p.op("sp", ins, writes=[TC], dma=len(ins))

    wq = []
    wstate = {"next_dma": 0, "next_use": 0}
    Txload = T("xload")

    def cn(src2d):
        return src2d.rearrange("(c p) n -> p c n", p=128)

    for l in layers:
        for h in range(NH):
            for j in range(4):
                c0 = j * 512 + h * 128
                wq.append(("cn", cn(w_in_d[l, :, c0:c0 + 128])))
        for hh in range(4):
            wq.append(("n", w_out_d[l, hh * 128:(hh + 1) * 128, :]))
        for h in range(NH):
            for j in range(3):
                c0 = 2048 + j * 512 + h * 128
                wq.append(("cn", cn(w_in_d[l, :, c0:c0 + 128])))
        for hh in range(4):
            wq.append(("n", w_out_d[l, 512 + hh * 128:512 + (hh + 1) * 128, :]))
        for g in range(NFC // FG):
            for f in range(g * FG, (g + 1) * FG):
                wq.append(("cn", cn(w_gate_d[l, :, f * 128:(f + 1) * 128])))
                wq.append(("cn", cn(w_up_d[l, :, f * 128:(f + 1) * 128])))
            for f in range(g * FG, (g + 1) * FG):
                wq.append(("n", w_down_d[l, f * 128:(f + 1) * 128, :]))

    def ring_view(slot, kind):
        ap = ring[slot]
        if kind == "cn":
            ap = ap.rearrange("p (c n) -> p c n", c=8)
        return ap

    def weight_done():
        k = wstate["next_dma"]
        if k >= len(wq):
            return
        kind, src = wq[k]
        slot = k % RING
        p.op("pool", [I("dma_start", out=ring_view(slot, kind), in_=src)], reads=([Txload] if k < RING else []), writes=[Tring[slot]], dma=1, nobar=True)
        wstate["next_dma"] = k + 1

    def next_weight(kind):
        k = wstate["next_use"]
        assert wq[k][0] == kind, (k, wq[k][0], kind)
        assert k < wstate["next_dma"]
        wstate["next_use"] = k + 1
        slot = k % RING
        return ring_view(slot, kind), Tring[slot]

    base_mark = A.mark()

    def phase_load_x():
        m = A.mark()
        xin = [A.alloc([D], F32) for _ in range(6)]
        Txin = [T("xin%d" % i) for i in range(6)]
        for b in range(NB):
            xi, txi = xin[b % 6], Txin[b % 6]
            p.op("sp", [I("dma_start", out=xi, in_=x_d[b * 128:(b + 1) * 128, :])], writes=[txi] + ([Txload] if b == NB - 3 else []), dma=1)
            for half in range(2):
                bank = (2 * b + half) % 8
                ins = [I("transpose", out=psb(bank)[:, j * 128:(j + 1) * 128],
                         in_=xi[:, (half * 4 + j) * 128:(half * 4 + j + 1) * 128], identity=C["ident_f"]) for j in range(4)]
                p.op("pe", ins, reads=[txi, TC], writes=[Tps[bank]])
                dst = xT[:, half * 4:half * 4 + 4, b * 128:(b + 1) * 128]
                src = psb(bank).rearrange("p (j t) -> p j t", j=4)
                wr = [Tx[c][b // 4] for c in range(half * 4, half * 4 + 4)]
                if half == 0:
                    p.op("act", [I("copy", out=dst, in_=src)], reads=[Tps[bank]], writes=wr)
                else:
                    p.op("dve", [I("tensor_copy", out=dst, in_=src)], reads=[Tps[bank]], writes=wr)
            if b == NB - 3:
                for _ in range(RING):
                    weight_done()
        p.barrier()
        A.release(m)

    def norm_scratch():
        sq = [A.alloc([TW], BF16) for _ in range(2)]
        rs = [A.alloc([TW], F32) for _ in range(2)]
        return sq, [T("sq0"), T("sq1")], rs, [T("rs0"), T("rs1")]

    def rmsnorm_tile(n, dst_fn, gcols, bank, scratch):
        sq, Tsq, rs, Trs = scratch
        tsl = slice(n * TW, (n + 1) * TW)
        for c in range(8):
            k = c % 2
            p.op("act", [I("activation", out=sq[k], in_=xT[:, c, tsl], func=AF.Square)], reads=[Tx[c][n]], writes=[Tsq[k]])
            p.op("pe", [I("matmul", psb(bank), lhsT=C["onesD"], rhs=sq[k], start=(c == 0), stop=(c == 7))],
                 reads=[Tsq[k], TC], writes=[Tps[bank]])
        r, tr_ = rs[n % 2], Trs[n % 2]
        p.op("act", [I("activation", out=r, in_=psb(bank), func=AF.Ln, bias=EPS)], reads=[Tps[bank]], writes=[tr_])
        p.op("act", [I("activation", out=r, in_=r, func=AF.Exp, scale=-0.5)], reads=[tr_], writes=[tr_])
        for c in range(8):
            d_, td_ = dst_fn(c)
            p.op("dve", [I("scalar_tensor_tensor", out=d_, in0=xT[:, c, tsl], scalar=gcols[:, c:c + 1], in1=r, op0=ALU.mult, op1=ALU.mult)],
                 reads=[Tx[c][n], tr_, TC], writes=[td_])

    def rmsnorm_h(gcols):
        m = A.mark()
        sc = norm_scratch()
        for n in range(NT):
            tsl = slice(n * TW, (n + 1) * TW)
            rmsnorm_tile(n, lambda c, n=n, tsl=tsl: (hT[:, c, tsl], Th[c][n]), gcols, n % 2, sc)
        p.barrier()
        A.release(m)

    def wout_half(catT, Tcat):
        ws = [next_weight("n") for _ in range(4)]
        i = 0
        for n in range(NT):
            for c in range(8):
                bank = i % 4
                i += 1
                tsl = slice(n * TW, (n + 1) * TW)
                ins = [I("matmul", psb(bank), lhsT=ws[hh][0][:, c * 128:(c + 1) * 128], rhs=catT[:, hh, tsl],
                         start=(hh == 0), stop=(hh == 3)) for hh in range(4)]
                p.op("pe", ins, reads=[w[1] for w in ws] + [Tcat[hh][n] for hh in range(4)], writes=[Tps[bank]])
                p.op("dve", [I("tensor_tensor", out=xT[:, c, tsl], in0=psb(bank), in1=xT[:, c, tsl], op=ALU.add)],
                     reads=[Tps[bank], Tx[c][n]], writes=[Tx[c][n]])
        for _ in range(4):
            weight_done()

    def phase_retention(l, catT, Tcat):
        m = A.mark()
        qkT = A.alloc([2, S], BF16)
        qdT = A.alloc([S], BF16)
        ktok = A.alloc([NB, 128], BF16); v = A.alloc([NB, 128], BF16); vd = A.alloc([NB, 128], BF16)
        gateT = A.alloc([S], BF16)
        Sbf = A.alloc([NB, 128], BF16)
        Sf = A.alloc([128], F32)
        tA = [A.alloc([2, 128], F32) for _ in range(2)]
        tB = [A.alloc([2, 128], F32) for _ in range(2)]
        qrot = [A.alloc([128], BF16) for _ in range(2)]
        smT = [A.alloc([4, 128], BF16) for _ in range(2)]
        obf = [A.alloc([TW], BF16) for _ in range(2)]
        osq = [A.alloc([TW], BF16) for _ in range(2)]
        rr = [A.alloc([TW], F32) for _ in range(2)]
        Tqk, TqdT, Tktok, Tv, Tvd, Tgate, TSbf, TSf = [T(n) for n in "qk qdT ktok v vd gate Sbf Sf".split()]
        TtA = [T("tA0"), T("tA1")]; TtB = [T("tB0"), T("tB1")]; Tqrot = [T("qr0"), T("qr1")]
        TsmT = [T("sm0"), T("sm1")]
        Tobf = [T("ob0"), T("ob1")]; Tosq = [T("os0"), T("os1")]; Trr = [T("rr0"), T("rr1")]
        qT = qkT[:, 0, :]
        kT = qkT[:, 1, :]

        for h in range(NH):
            (wq_, Twq), (wk_, Twk), (wv_, Twv), (wg_, Twg) = [next_weight("cn") for _ in range(4)]

            def qkv(b):
                bank = 2 + (b % 2)
                bsl = slice(b * 128, (b + 1) * 128)
                ins = []
                if (wk_.offset - wq_.offset == 1024) and (wv_.offset - wk_.offset == 1024):
                    out3 = psb(bank)[:, 0:384].rearrange("p (j n) -> p j n", j=3)
                    for c in range(8):
                        ins.append(I("matmul", out3, lhsT=hT[:, c, bsl], rhs=mkap(wq_[:, c, :], 0, [[1024, 3], [1, 128]]),
                                     start=(c == 0), stop=(c == 7)))
                else:
                    for j, w_ in enumerate((wq_, wk_, wv_)):
                        for c in range(8):
                            ins.append(I("matmul", psb(bank)[:, j * 128:(j + 1) * 128], lhsT=hT[:, c, bsl], rhs=w_[:, c, :],
                                         start=(c == 0), stop=(c == 7)))
                p.op("pe", ins, reads=[Twq, Twk, Twv] + [Th[c][b // 4] for c in range(8)], writes=[Tps[bank]])

            def gate_tile(n):
                bank = n % 2
                tsl = slice(n * TW, (n + 1) * TW)
                ins = [I("matmul", psb(bank), lhsT=wg_[:, c, :], rhs=hT[:, c, tsl], start=(c == 0), stop=(c == 7)) for c in range(8)]
                p.op("pe", ins, reads=[Twg] + [Th[c][n] for c in range(8)], writes=[Tps[bank]])
                p.op("act", [I("activation", out=gateT[:, tsl], in_=psb(bank), func=AF.Silu)], reads=[Tps[bank]], writes=[Tgate])

            def evac(b):
                tbank = 4 + (b % 2)
                bsl = slice(b * 128, (b + 1) * 128)
                p.op("dve", [I("tensor_copy", out=qkT[:, :, bsl], in_=psb16(tbank)[:, 0:256].rearrange("p (j t) -> p j t", j=2))],
                     reads=[Tps[tbank]], writes=[Tqk])
                p.op("dve", [I("tensor_tensor", out=qdT[:, bsl], in0=psb16(tbank)[:, 0:128], in1=C["qdec"][:, h, :], op=ALU.mult)],
                     reads=[Tps[tbank], TC], writes=[TqdT])

            p.op("pool", [I("memset", Sf, 0.0)], writes=[TSf])
            p.op("pool", [I("memset", Sbf[:, 0, :], 0.0)], writes=[TSbf])
            qkv(0)
            for b in range(NB):
                if b + 1 < NB:
                    qkv(b + 1)
                bank = 2 + (b % 2)
                a_, ta_ = tA[b % 2], TtA[b % 2]
                b_, tb_ = tB[b % 2], TtB[b % 2]
                xq = psb(bank)[:, 0:256]
                x4 = mkap(xq, 0, [[128, 2], [64, 2], [1, 64]])
                xsw = mkap(xq, 64, [[128, 2], [-64, 2], [1, 64]])
                cos4 = mkap(C["cos"][:, b, :], 0, [[0, 2], [0, 2], [1, 64]])
                sin4 = mkap(C["sins"][:, b, :, :], 0, [[0, 2], [64, 2], [1, 64]])
                a4 = a_.rearrange("p q (a b) -> p q a b", a=2)
                b4 = b_.rearrange("p q (a b) -> p q a b", a=2)
                p.op("dve", [I("tensor_tensor", out=a4, in0=x4, in1=cos4, op=ALU.mult)], reads=[Tps[bank], TC], writes=[ta_])
                p.op("dve", [I("tensor_tensor", out=b4, in0=xsw, in1=sin4, op=ALU.mult)], reads=[Tps[bank], TC], writes=[tb_])
                qr, tqr = qrot[b % 2], Tqrot[b % 2]
                p.op("pool", [I("tensor_tensor", out=qr, in0=a_[:, 0, :], in1=b_[:, 0, :], op=ALU.add)], reads=[ta_, tb_], writes=[tqr])
                p.op("pool", [I("tensor_tensor", out=ktok[:, b, :], in0=a_[:, 1, :], in1=b_[:, 1, :], op=ALU.add)], reads=[ta_, tb_], writes=[Tktok])
                p.op("dve", [I("tensor_copy", out=v[:, b, :], in_=psb(bank)[:, 256:384])], reads=[Tps[bank]], writes=[Tv])
                p.op("dve", [I("tensor_scalar_mul", out=vd[:, b, :], in0=psb(bank)[:, 256:384], scalar1=C["kdec"][:, h:h + 1])],
                     reads=[Tps[bank], TC], writes=[Tvd])
                tbank = 4 + (b % 2)
                ins = [I("transpose", out=psb16(tbank)[:, 0:128], in_=qr, identity=C["ident_bf"]),
                       I("transpose", out=psb16(tbank)[:, 128:256], in_=ktok[:, b, :], identity=C["ident_bf"])]
                p.op("pe", ins, reads=[tqr, Tktok, TC], writes=[Tps[tbank]])
                if b >= 1:
                    evac(b - 1)
                    pbank = 6 + ((b - 1) % 2)
                    p.op("dve", [I("scalar_tensor_tensor", out=Sf, in0=Sf, scalar=CD[h], in1=psb(pbank)[:, 0:128], op0=ALU.mult, op1=ALU.add)],
                         reads=[TSf, Tps[pbank]], writes=[TSf])
                    p.op("pool", [I("tensor_copy", out=Sbf[:, b, :], in_=Sf)], reads=[TSf], writes=[TSbf])
                if b < NB - 1:
                    ubank = 6 + (b % 2)
                    p.op("pe", [I("matmul", psb(ubank)[:, 0:128], lhsT=ktok[:, b, :], rhs=vd[:, b, :], start=True, stop=True)],
                         reads=[Tktok, Tvd], writes=[Tps[ubank]])
                if b % 4 == 2:
                    gate_tile(b // 4)
            evac(NB - 1)
            for _ in range(4):
                weight_done()
            def tile_s1(n):
                sbank = n % 2
                obank = 2 + (n % 2)
                ins = []
                for q in range(4):
                    bsl = slice((4 * n + q) * 128, (4 * n + q + 1) * 128)
                    ins.append(I("matmul", psb(sbank)[:, q * 128:(q + 1) * 128], lhsT=kT[:, bsl], rhs=qT[:, bsl], start=True, stop=True))
                p.op("pe", ins, reads=[Tqk], writes=[Tps[sbank]])
                sm, tsm = smT[n % 2], TsmT[n % 2]
                mk3 = mkap(C["maskT"][:, h, :], 0, [[0, 4], [1, 128]])
                p.op("dve", [I("tensor_tensor", out=sm, in0=psb(sbank).rearrange("p (q i) -> p q i", q=4), in1=mk3, op=ALU.mult)],
                     reads=[Tps[sbank], TC], writes=[tsm])

            def tile_s1b(n):
                obank = 2 + (n % 2)
                sm, tsm = smT[n % 2], TsmT[n % 2]
                ins = []
                for q in range(4):
                    blk = 4 * n + q
                    bsl = slice(blk * 128, (blk + 1) * 128)
                    ins.append(I("matmul", psb(obank)[:, q * 128:(q + 1) * 128], lhsT=v[:, blk, :], rhs=sm[:, q, :], start=True, stop=False))
                    ins.append(I("matmul", psb(obank)[:, q * 128:(q + 1) * 128], lhsT=Sbf[:, blk, :], rhs=qdT[:, bsl], start=False, stop=True))
                p.op("pe", ins, reads=[Tv, tsm, TSbf, TqdT], writes=[Tps[obank]])
                p.op("act", [I("activation", out=obf[n % 2], in_=psb(obank), func=AF.Copy, scale=SCALE)], reads=[Tps[obank]], writes=[Tobf[n % 2]])

            def tile_s2(n):
                cbank = 4 + (n % 2)
                p.op("pe", [I("matmul", psb(cbank), lhsT=C["cenM"], rhs=obf[n % 2], start=True, stop=True)], reads=[Tobf[n % 2], TC], writes=[Tps[cbank]])
                p.op("act", [I("activation", out=osq[n % 2], in_=psb(cbank), func=AF.Square)], reads=[Tps[cbank]], writes=[Tosq[n % 2]])

            def tile_s3(n):
                tsl = slice(n * TW, (n + 1) * TW)
                cbank = 4 + (n % 2)
                vbank = 6 + (n % 2)
                r_, tr_ = rr[n % 2], Trr[n % 2]
                p.op("pe", [I("matmul", psb(vbank), lhsT=C["onesH"], rhs=osq[n % 2], start=True, stop=True)], reads=[Tosq[n % 2], TC], writes=[Tps[vbank]])
                p.op("act", [I("activation", out=r_, in_=psb(vbank), func=AF.Ln, bias=EPS)], reads=[Tps[vbank]], writes=[tr_])
                p.op("act", [I("activation", out=r_, in_=r_, func=AF.Exp, scale=-0.5)], reads=[tr_], writes=[tr_])
                p.op("pool", [I("tensor_tensor", out=r_, in0=r_, in1=gateT[:, tsl], op=ALU.mult)], reads=[tr_, Tgate], writes=[tr_])
                gcol = gvec["rg"][:, l * NH + h:l * NH + h + 1]
                p.op("dve", [I("scalar_tensor_tensor", out=catT[:, h, tsl], in0=psb(cbank), scalar=gcol, in1=r_, op0=ALU.mult, op1=ALU.mult)],
                     reads=[Tps[cbank], tr_, TC], writes=[Tcat[h][n]])

            for k in range(NT + 3):
                if k < NT:
                    tile_s1(k)
                if 0 <= k - 1 < NT:
                    tile_s1b(k - 1)
                if 0 <= k - 2 < NT:
                    tile_s2(k - 2)
                if 0 <= k - 3 < NT:
                    tile_s3(k - 3)
        p.barrier()
        A.release(m)

    def phase_sb(l, catT, Tcat):
        m = A.mark()
        NE, NW = 4, 3
        qTb = [A.alloc([S], BF16) for _ in range(2)]
        kTb = [A.alloc([S], BF16) for _ in range(2)]
        vb = [A.alloc([NB, 128], BF16) for _ in range(2)]
        TqT = [T("sqT0"), T("sqT1")]; TkT = [T("skT0"), T("skT1")]; Tv = [T("sv0"), T("sv1")]
        Gb = [A.alloc([520], F32) for _ in range(NE)]
        Qb = [A.alloc([520], F32) for _ in range(NE)]
        wb = [A.alloc([512], BF16) for _ in range(NW)]
        wTb = [A.alloc([4, 128], BF16) for _ in range(2)]
        carry = [A.alloc([1], F32) for _ in range(2)]
        osq = [A.alloc([TW], BF16) for _ in range(2)]
        rs = [A.alloc([TW], F32) for _ in range(2)]
        TG = [T("G%d" % i) for i in range(NE)]; TQ = [T("Q%d" % i) for i in range(NE)]
        for i in range(NE):
            p.op("pool", [I("memset", Gb[i][:, 512:513], 1.0)], writes=[TG[i]])
        Tw = [T("w%d" % i) for i in range(NW)]; TwT = [T("wT0"), T("wT1")]
        Tcarry = [T("cy0"), T("cy1")]; Tosq = [T("sos0"), T("sos1")]; Trs = [T("srs0"), T("srs1")]
        ZB = [0, 1, 2]
        PB = 3
        TB = [4, 5]
        OB = [6, 7]

        def proj_pieces(h):
            par = h % 2
            ws = [next_weight("cn") for _ in range(3)]
            (wq_, Twq), (wk_, Twk), (wv_, Twv) = ws
            pieces = []
            for j, (w_, tw_, dst, tdst) in enumerate(((wq_, Twq, qTb[par], TqT[par]), (wk_, Twk, kTb[par], TkT[par]))):
                for n in range(NT):
                    def piece(j=j, w_=w_, tw_=tw_, dst=dst, tdst=tdst, n=n):
                        tsl = slice(n * TW, (n + 1) * TW)
                        ins = [I("matmul", psb(PB), lhsT=w_[:, c, :], rhs=hT[:, c, tsl], start=(c == 0), stop=(c == 7)) for c in range(8)]
                        p.op("pe", ins, reads=[tw_] + [Th[c][n] for c in range(8)], writes=[Tps[PB]])
                        if j == 0:
                            p.op("act", [I("activation", out=dst[:, tsl], in_=psb(PB), func=AF.Copy, scale=SCALE)], reads=[Tps[PB]], writes=[tdst])
                        else:
                            p.op("act", [I("copy", out=dst[:, tsl], in_=psb(PB))], reads=[Tps[PB]], writes=[tdst])
                    pieces.append(piece)
            for g4 in range(4):
                def piece(g4=g4, wv_=wv_, Twv=Twv, par=par):
                    ins = []
                    for q in range(4):
                        bsl = slice((4 * g4 + q) * 128, (4 * g4 + q + 1) * 128)
                        for c in range(8):
                            ins.append(I("matmul", psb(PB)[:, q * 128:(q + 1) * 128], lhsT=hT[:, c, bsl], rhs=wv_[:, c, :], start=(c == 0), stop=(c == 7)))
                    p.op("pe", ins, reads=[Twv] + [Th[c][g4] for c in range(8)], writes=[Tps[PB]])
                    p.op("act", [I("copy", out=vb[par][:, 4 * g4:4 * g4 + 4, :], in_=psb(PB).rearrange("p (q e) -> p q e", q=4))],
                         reads=[Tps[PB]], writes=[Tv[par]])
                pieces.append(piece)

            def done():
                for _ in range(3):
                    weight_done()
            pieces.append(done)
            return pieces

        units = []
        head_start = []
        for h in range(NH):
            head_start.append(len(units))
            for Tq in range(NB):
                nk = 128 * (Tq + 1)
                hi = nk
                first = True
                while hi > 0:
                    lo = max(0, hi - 512)
                    units.append((h, Tq, lo, hi, first, lo == 0))
                    first = False
                    hi = lo
        U = len(units)

        def st_z(u):
            h, Tq, k0, k1, first, last = units[u]
            par = h % 2
            n_ = k1 - k0
            zb = ZB[u % 3]
            G, tG = Gb[u % NE], TG[u % NE]
            qsl = slice(Tq * 128, (Tq + 1) * 128)
            ins = [I("matmul", psb(zb)[:, 0:n_], lhsT=qTb[par][:, qsl], rhs=kTb[par][:, k0:k1], start=True, stop=not first)]
            if first:
                ins.append(I("matmul", psb(zb)[:, n_ - 128:n_], lhsT=C["ident_bf"], rhs=C["negmask"], start=False, stop=True))
            p.op("pe", ins, reads=[TqT[par], TkT[par], TC], writes=[Tps[zb]])
            p.op("act", [I("activation", out=G[:, 512 - n_:512], in_=psb(zb)[:, 0:n_], func=AF.Sigmoid, scale=-1.0)], reads=[Tps[zb]], writes=[tG])

        def st_scan(u):
            h, Tq, k0, k1, first, last = units[u]
            n_ = k1 - k0
            G, tG = Gb[u % NE], TG[u % NE]
            Q, tQ = Qb[u % NE], TQ[u % NE]
            gr = mkap(G, 512, [[-1, n_ + 1]])
            qr = mkap(Q, 512, [[-1, n_ + 1]])
            onr = mkap(C["ones_col"], 0, [[0, n_ + 1]])
            if first:
                p.op("dve", [I("tensor_tensor_scan", out=qr, data0=gr, data1=onr, initial=1.0, op0=ALU.mult, op1=ALU.mult)],
                     reads=[tG, TC], writes=[tQ])
            else:
                pu = u - 1
                pn = units[pu][3] - units[pu][2]
                cy = Qb[pu % NE][:, 512 - pn:513 - pn]
                p.op("dve", [I("tensor_tensor_scan", out=qr, data0=gr, data1=onr, initial=cy, op0=ALU.mult, op1=ALU.mult)],
                     reads=[tG, TC, TQ[pu % NE]], writes=[tQ])

        def st_w(u):
            h, Tq, k0, k1, first, last = units[u]
            n_ = k1 - k0
            Q, tQ = Qb[u % NE], TQ[u % NE]
            w_, tw_ = wb[u % NW], Tw[u % NW]
            p.op("pool", [I("tensor_tensor", out=w_[:, 0:n_], in0=Q[:, 513 - n_:513], in1=Q[:, 512 - n_:512], op=ALU.subtract)], reads=[tQ], writes=[tw_])

        def st_tr(u):
            h, Tq, k0, k1, first, last = units[u]
            nkb = (k1 - k0) // 128
            w_, tw_ = wb[u % NW], Tw[u % NW]
            wT, twT = wTb[u % 2], TwT[u % 2]
            tb = TB[u % 2]
            ins = [I("transpose", out=psb16(tb)[:, i * 128:(i + 1) * 128], in_=w_[:, i * 128:(i + 1) * 128], identity=C["ident_bf"]) for i in range(nkb)]
            p.op("pe", ins, reads=[tw_, TC], writes=[Tps[tb]])
            p.op("act", [I("copy", out=wT[:, 0:nkb, :], in_=psb16(tb)[:, 0:nkb * 128].rearrange("p (i t) -> p i t", i=nkb))],
                 reads=[Tps[tb]], writes=[twT])

        def st_av(u):
            h, Tq, k0, k1, first, last = units[u]
            par = h % 2
            nkb = (k1 - k0) // 128
            wT, twT = wTb[u % 2], TwT[u % 2]
            ob = OB[(Tq // 4) % 2]
            q = Tq % 4
            ins = [I("matmul", psb(ob)[:, q * 128:(q + 1) * 128], lhsT=vb[par][:, k0 // 128 + i, :], rhs=wT[:, i, :],
                     start=(first and i == 0), stop=(last and i == nkb - 1)) for i in range(nkb)]
            p.op("pe", ins, reads=[Tv[par], twT], writes=[Tps[ob]])
            if last and q == 3:
                n = Tq // 4
                tsl = slice(n * TW, (n + 1) * TW)
                oq, toq = osq[n % 2], Tosq[n % 2]
                r_, tr_ = rs[n % 2], Trs[n % 2]
                mbank = PB
                gcol = gvec["sg"][:, l * NH + h:l * NH + h + 1]

                def n1():
                    p.op("act", [I("activation", out=oq, in_=psb(ob), func=AF.Square)], reads=[Tps[ob]], writes=[toq])

                def n2():
                    p.op("pe", [I("matmul", psb(mbank), lhsT=C["onesH"], rhs=oq, start=True, stop=True)], reads=[toq, TC], writes=[Tps[mbank]])

                def n3():
                    p.op("act", [I("activation", out=r_, in_=psb(mbank), func=AF.Ln, bias=EPS)], reads=[Tps[mbank]], writes=[tr_])
                    p.op("act", [I("activation", out=r_, in_=r_, func=AF.Exp, scale=-0.5)], reads=[tr_], writes=[tr_])

                def n4():
                    p.op("dve", [I("scalar_tensor_tensor", out=catT[:, h, tsl], in0=psb(ob), scalar=gcol, in1=r_, op0=ALU.mult, op1=ALU.mult)],
                         reads=[Tps[ob], tr_, TC], writes=[Tcat[h][n]])
                deferred.setdefault(cur_it[0] + 1, []).append(n1)
                deferred.setdefault(cur_it[0] + 2, []).append(n2)
                deferred_end.setdefault(cur_it[0] + 2, []).append(n3)
                deferred.setdefault(cur_it[0] + 3, []).append(n4)

        for pc in proj_pieces(0):
            pc()
        pending = []
        deferred = {}
        deferred_end = {}
        cur_it = [0]
        for i in range(U + 10):
            cur_it[0] = i
            for f in deferred.pop(i, []):
                f()
            if 0 <= i - 4 < U:
                st_tr(i - 4)
            if 0 <= i - 5 < U:
                st_av(i - 5)
            if i < U:
                h = units[i][0]
                if i == head_start[h] and h + 1 < NH:
                    pending = proj_pieces(h + 1)
                st_z(i)
            if 0 <= i - 2 < U:
                st_w(i - 2)
            if 0 <= i - 1 < U:
                st_scan(i - 1)
            for f in deferred_end.pop(i, []):
                f()
            if pending and i < U and (i - head_start[units[i][0]]) % 3 == 2:
                pending.pop(0)()
        assert not pending and not deferred and not deferred_end
        p.barrier()
        A.release(m)

    def phase_ffn(l):
        m = A.mark()
        actT = A.alloc([FG, S], BF16)
        Tact = [[T("a%d_%d" % (f, n)) for n in range(NT)] for f in range(FG)]
        sg = [A.alloc([TW], BF16) for _ in range(2)]
        Tsg = [T("sg0"), T("sg1")]
        i = 0
        for g in range(NFC // FG):
            for fl in range(FG):
                (wg_, Twg), (wu_, Twu) = next_weight("cn"), next_weight("cn")
                for n in range(NT):
                    tsl = slice(n * TW, (n + 1) * TW)
                    gb = (2 * i) % 8
                    ub = (2 * i + 1) % 8
                    i += 1
                    for (w_, tw_, bank) in ((wg_, Twg, gb), (wu_, Twu, ub)):
                        ins = [I("matmul", psb(bank), lhsT=w_[:, c, :], rhs=hT[:, c, tsl], start=(c == 0), stop=(c == 7)) for c in range(8)]
                        p.op("pe", ins, reads=[tw_] + [Th[c][n] for c in range(8)], writes=[Tps[bank]])
                    s_, ts_ = sg[i % 2], Tsg[i % 2]
                    p.op("act", [I("activation", out=s_, in_=psb(gb), func=AF.Silu)], reads=[Tps[gb]], writes=[ts_])
                    p.op("dve", [I("tensor_tensor", out=actT[:, fl, tsl], in0=psb(ub), in1=s_, op=ALU.mult)], reads=[Tps[ub], ts_], writes=[Tact[fl][n]])
                weight_done()
                weight_done()
            wds = [next_weight("n") for _ in range(FG)]
            for c in range(8):
                for n in range(NT):
                    tsl = slice(n * TW, (n + 1) * TW)
                    bank = i % 8
                    i += 1
                    ins = [I("matmul", psb(bank), lhsT=wds[fl][0][:, c * 128:(c + 1) * 128], rhs=actT[:, fl, tsl], start=(fl == 0), stop=(fl == FG - 1))
                           for fl in range(FG)]
                    p.op("pe", ins, reads=[w[1] for w in wds] + [Tact[fl][n] for fl in range(FG)], writes=[Tps[bank]])
                    p.op("dve", [I("tensor_tensor", out=xT[:, c, tsl], in0=psb(bank), in1=xT[:, c, tsl], op=ALU.add)],
                         reads=[Tps[bank], Tx[c][n]], writes=[Tx[c][n]])
            for _ in range(FG):
                weight_done()
        p.barrier()
        A.release(m)

    def phase_out(do_norm):
        m = A.mark()
        sc = norm_scratch()
        yt = [A.alloc([8, TW], F32) for _ in range(2)]
        Tyt = [[T("yt%d_%d" % (k, c)) for c in range(8)] for k in range(2)]
        yo = [A.alloc([D], F32) for _ in range(4)]
        Tyo = [T("yo%d" % i) for i in range(4)]
        Tout = T("out")

        def norm(n):
            k = n % 2
            rmsnorm_tile(n, lambda c, k=k: (yt[k][:, c, :], Tyt[k][c]), gvec["fg"], n % 2, sc)

        def trans(n):
            for bq in range(4):
                b = 4 * n + bq
                y_, ty_ = yo[b % 4], Tyo[b % 4]
                for half in range(2):
                    bank = 2 + (2 * b + half) % 6
                    ins = []
                    rd = [TC]
                    for j in range(4):
                        c = half * 4 + j
                        if do_norm:
                            src = yt[n % 2][:, c, bq * 128:(bq + 1) * 128]
                            rd.append(Tyt[n % 2][c])
                        else:
                            src = xT[:, c, b * 128:(b + 1) * 128]
                            rd.append(Tx[c][n])
                        ins.append(I("transpose", out=psb(bank)[:, j * 128:(j + 1) * 128], in_=src, identity=C["ident_f"]))
                    p.op("pe", ins, reads=rd, writes=[Tps[bank]])
                    if half == 0:
                        p.op("act", [I("copy", out=y_[:, 0:512], in_=psb(bank))], reads=[Tps[bank]], writes=[ty_])
                    else:
                        p.op("dve", [I("tensor_copy", out=y_[:, 512:1024], in_=psb(bank))], reads=[Tps[bank]], writes=[ty_])
                p.op("sp", [I("dma_start", out=y_d[b * 128:(b + 1) * 128, :], in_=y_)], reads=[ty_], writes=[Tout], dma=1)

        if do_norm:
            norm(0)
        for n in range(NT):
            if do_norm and n + 1 < NT:
                norm(n + 1)
            trans(n)
        p.op("sp", None, reads=[Tout])
        A.release(m)

    import os
    KSTOP = os.environ.get("KSTOP", "")
    phase_load_x()
    stopped = False
    for l in layers:
        m = A.mark()
        catT = A.alloc([4, S], BF16)
        Tcat = [[T("cat%d_%d" % (hh, n)) for n in range(NT)] for hh in range(4)]
        rmsnorm_h(gvec["n1"][:, l * 8:(l + 1) * 8])
        if KSTOP == "norm1":
            stopped = True
            break
        phase_retention(l, catT, Tcat)
        if KSTOP == "ret":
            stopped = True
            break
        wout_half(catT, Tcat)
        if KSTOP == "wout1":
            stopped = True
            break
        phase_sb(l, catT, Tcat)
        if KSTOP == "sb":
            stopped = True
            break
        wout_half(catT, Tcat)
        if KSTOP == "wout2":
            p.barrier()
            stopped = True
            break
        rmsnorm_h(gvec["n2"][:, l * 8:(l + 1) * 8])
        A.release(m)
        phase_ffn(l)
    if stopped:
        A.release(base_mark)
    phase_out(final and not stopped)
    if not stopped:
        assert wstate["next_use"] == len(wq), (wstate, len(wq))
    print("SBUF arena peak bytes/partition:", A.peak, "ops:", {e: len(p.ops[e]) for e in p.ENG})
    p.emit()
    return nc


_PROG_CACHE = {}


def _get_prog(key):
    if key not in _PROG_CACHE:
        _PROG_CACHE[key] = build_program(*key)
    return _PROG_CACHE[key]


def _colsT(g, n):
    g = np.asarray(g, dtype=np.float32).reshape(-1, n, 128)
    return np.ascontiguousarray(g.transpose(2, 0, 1).reshape(128, -1))


FUSED = True


def kernel(x, norm1_g, w_in, ret_norm_g, sb_norm_g, w_out, norm2_g, w_gate, w_up, w_down, final_g):
    consts, _ = make_consts()
    shared = {
        "norm1_gT": _colsT(norm1_g, 8), "norm2_gT": _colsT(norm2_g, 8),
        "ret_gT": _colsT(ret_norm_g, NH), "sb_gT": _colsT(sb_norm_g, NH),
        "final_gT": _colsT(final_g, 8),
        "w_in": np.ascontiguousarray(w_in, dtype=np.float32), "w_out": np.ascontiguousarray(w_out, dtype=np.float32),
        "w_gate": np.ascontiguousarray(w_gate, dtype=np.float32), "w_up": np.ascontiguousarray(w_up, dtype=np.float32),
        "w_down": np.ascontiguousarray(w_down, dtype=np.float32),
    }
    for name, _, _ in CONST_SPECS:
        shared["c_" + name] = consts[name]
    xs = [np.ascontiguousarray(x[i], dtype=np.float32) for i in range(NCORES)]
    if FUSED:
        stages = [((0, 1), True, True)]
    else:
        stages = [((0,), True, False), ((1,), True, True)]
    for (layers, first, final) in stages:
        nc = _get_prog((tuple(layers), first, final))
        in_maps = [dict(shared, x=xs[i]) for i in range(NCORES)]
        res = run_bass_kernel_spmd(nc, in_maps, core_ids=list(range(NCORES)))
        xs = [np.asarray(res.results[i]["y"], dtype=np.float32) for i in range(NCORES)]
    return np.stack(xs, axis=0).astype(np.float32)
```

```python
import contextlib
import numpy as np
import ml_dtypes
import concourse.bass as bass
import concourse.mybir as mybir
from concourse.ap import AP
from concourse.bass_utils import run_bass_kernel_spmd

F32 = mybir.dt.float32
BF16 = mybir.dt.bfloat16
ALU = mybir.AluOpType
AF = mybir.ActivationFunctionType

D = 1024
S = 2048
DEPTH = 2
HD = 128
NH = 4
DFF = 2816
NFC = DFF // 128
NB = S // 128
NT = 4
TW = 512
INW = 3584
EPS = 1e-6
NCORES = 8
RING = 14
ARENA_BYTES = 212480
FG = 11
SCALE = float(HD ** -0.5)


class T:
    __slots__ = ("name", "w", "rc", "rd")

    def __init__(self, name=""):
        self.name = name
        self.w = None
        self.rc = {}
        self.rd = []


class Op:
    __slots__ = ("eng", "fn", "deps", "idx", "signal", "cnt", "dma", "dsem", "dtarget", "dprev")


class Prog:
    ENG = ("pe", "act", "dve", "pool", "sp")

    def __init__(self, nc, ndsem=12):
        self.nc = nc
        self.ops = {e: [] for e in self.ENG}
        self.ndsem = ndsem
        self.pending_bar = {}
        self.dma_since_bar = []

    def barrier(self):
        deps = []
        for e in self.ENG:
            for o in reversed(self.ops[e]):
                if not o.dma:
                    deps.append(o)
                    break
        deps.extend(self.dma_since_bar)
        self.dma_since_bar = []
        for e in self.ENG:
            self.pending_bar[e] = list(deps) + self.pending_bar.get(e, [])

    def op(self, eng, fn, reads=(), writes=(), dma=0, nobar=False):
        o = Op()
        o.eng = eng
        o.fn = fn
        o.dma = dma
        o.signal = False
        o.cnt = 0
        deps = set()
        for t in reads:
            if t.w is not None:
                deps.add(t.w)
        for t in writes:
            if t.w is not None:
                deps.add(t.w)
            deps.update(t.rc.values())
            deps.update(t.rd)
        for t in reads:
            if dma:
                t.rd.append(o)
            else:
                t.rc[eng] = o
        for t in writes:
            t.w = o
            t.rc = {}
            t.rd = []
        deps.discard(o)
        if eng == "pe" and not dma:
            deps = {d for d in deps if d.dma or d.eng != "pe"}
        if eng in self.pending_bar:
            deps.update(self.pending_bar.pop(eng))
        best = {}
        out = []
        for d in deps:
            if d.dma:
                out.append(d)
            else:
                b = best.get(d.eng)
                if b is None or d.idx > b.idx:
                    best[d.eng] = d
        out.extend(best.values())
        o.deps = out
        o.idx = len(self.ops[eng])
        self.ops[eng].append(o)
        if dma and not nobar:
            self.dma_since_bar.append(o)
        return o

    def emit(self):
        nc = self.nc
        for e in self.ENG:
            for o in self.ops[e]:
                for d in o.deps:
                    d.signal = True
        for e in self.ENG:
            c = 0
            for o in self.ops[e]:
                if not o.dma and o.signal:
                    c += 1
                o.cnt = c
        with contextlib.ExitStack() as st:
            esem = {e: st.enter_context(nc.semaphore("es_" + e)) for e in self.ENG}
            dsem = {e: [st.enter_context(nc.semaphore("ds_%s_%d" % (e, i))) for i in range(self.ndsem)]
                    for e in ("sp", "pool", "act")}
            for e in self.ENG:
                k = 0
                targets = [0] * self.ndsem
                for o in self.ops[e]:
                    if o.dma:
                        slot = k % self.ndsem
                        o.dsem = dsem[e][slot]
                        o.dprev = targets[slot]
                        targets[slot] += 16 * o.dma
                        o.dtarget = targets[slot]
                        k += 1
            block = st.enter_context(nc.Block())

            def run(eng_name):
                def body(e):
                    known = {}
                    for o in self.ops[eng_name]:
                        waits = {}
                        for d in o.deps:
                            if d.dma:
                                key, val = d.dsem, d.dtarget
                            else:
                                key, val = esem[d.eng], d.cnt
                            if waits.get(key, (None, 0))[1] < val:
                                waits[key] = (key, val)
                        if o.dma and o.dprev > 0:
                            if waits.get(o.dsem, (None, 0))[1] < o.dprev:
                                waits[o.dsem] = (o.dsem, o.dprev)
                        for key, val in waits.values():
                            if known.get(key, 0) < val:
                                e.wait_ge(key, val)
                                known[key] = val
                        if o.fn is None:
                            continue
                        rs_ = [getattr(e, nm)(*a, **kw) for (nm, a, kw) in o.fn]
                        if o.dma:
                            assert len(rs_) == o.dma
                            for ins in rs_:
                                ins.then_inc(o.dsem, 16)
                        elif o.signal:
                            rs_[-1].then_inc(esem[eng_name], 1)
                return body

            block.tensor(run("pe"))
            block.scalar(run("act"))
            block.vector(run("dve"))
            block.gpsimd(run("pool"))
            block.sync(run("sp"))


class Arena:
    def __init__(self, nc, nbytes):
        self.t = nc.alloc_sbuf_tensor("arena", [128, nbytes // 2], BF16)
        self.top = 0
        self.cap = nbytes
        self.peak = 0

    def alloc(self, free, dtype):
        n = int(np.prod(free))
        sz = n * (4 if dtype == F32 else 2)
        off = self.top
        self.top += (sz + 63) // 64 * 64
        self.peak = max(self.peak, self.top)
        assert self.top <= self.cap, "SBUF arena overflow %d > %d" % (self.top, self.cap)
        ap = self.t[:, off // 2: off // 2 + sz // 2]
        if dtype == F32:
            ap = ap.bitcast(F32)
        if len(free) == 2:
            ap = ap.rearrange("p (a b) -> p a b", a=free[0])
        elif len(free) == 3:
            ap = ap.rearrange("p (a b c) -> p a b c", a=free[0], b=free[1])
        return ap

    def mark(self):
        return self.top

    def release(self, m):
        self.top = m


def I(name, *a, **kw):
    return (name, a, kw)


def mkap(base, extra, dims):
    return AP(base.tensor, base.offset + extra, [list(base.ap[0])] + [list(d) for d in dims])


def make_consts():
    c = {}
    c["ident_bf"] = np.eye(128, dtype=np.float32).astype(ml_dtypes.bfloat16)
    c["ident_f"] = np.eye(128, dtype=np.float32)
    c["onesD"] = np.full((128, 128), 1.0 / D, dtype=np.float32).astype(ml_dtypes.bfloat16)
    c["onesH"] = np.full((128, 128), 1.0 / HD, dtype=np.float32).astype(ml_dtypes.bfloat16)
    inv_freq = (1.0 / (np.float32(10000.0) ** (np.arange(0, HD, 2, dtype=np.float32) / np.float32(HD)))).astype(np.float32)
    pos = np.arange(S, dtype=np.float64)
    ang = pos[:, None] * inv_freq[None, :].astype(np.float64)
    cos = np.cos(ang).astype(np.float32).reshape(NB, 128, 64).transpose(1, 0, 2)
    sin = np.sin(ang).astype(np.float32).reshape(NB, 128, 64).transpose(1, 0, 2)
    c["cos"] = np.ascontiguousarray(cos)
    c["sins"] = np.ascontiguousarray(np.stack([-sin, sin], axis=2))
    lg = np.log1p(-np.exp2(-5.0 - np.arange(NH, dtype=np.float64)))
    i = np.arange(128)
    ii, jj = i[None, :], i[:, None]
    ci, cj = ii // 64, jj // 64
    maskT = np.zeros((128, NH, 128), dtype=np.float64)
    for h in range(NH):
        same = np.exp(lg[h] * np.abs(ii - jj))
        prev = np.exp(lg[h] * (ii - jj).clip(min=0))
        maskT[:, h, :] = np.where(ci == cj, same, np.where(cj < ci, prev, 0.0))
    c["maskT"] = maskT.astype(np.float32)
    qd = np.exp(lg[:, None] * (i[None, :] + 1.0))
    c["qdec"] = np.ascontiguousarray(np.broadcast_to(qd[None], (128, NH, 128))).astype(np.float32)
    c["kdec"] = np.exp(lg[None, :] * (127.0 - i[:, None])).astype(np.float32)
    cd = [float(np.exp(lg[h] * 128.0)) for h in range(NH)]
    t = np.arange(128)
    c["mask01"] = (t[None, :] < t[:, None]).astype(np.float32)
    c["ones_col"] = np.ones((128, 1), dtype=np.float32)
    c["negmask"] = (np.float32(-30000.0) * (t[None, :] >= t[:, None]).astype(np.float32)).astype(ml_dtypes.bfloat16)
    c["mhalf"] = np.full((128, 1), -0.5, dtype=np.float32)
    c["cenM"] = (np.eye(128, dtype=np.float32) - np.float32(1.0 / HD)).astype(ml_dtypes.bfloat16)
    c["notmask"] = (t[None, :] >= t[:, None]).astype(np.float32)
    return c, cd


CONST_SPECS = [
    ("ident_bf", [128], BF16), ("ident_f", [128], F32), ("onesD", [128], BF16), ("onesH", [128], BF16),
    ("cos", [NB, 64], F32), ("sins", [NB, 2, 64], F32), ("maskT", [NH, 128], F32), ("qdec", [NH, 128], F32),
    ("kdec", [NH], F32), ("notmask", [128], F32), ("mask01", [128], F32), ("ones_col", [1], F32), ("cenM", [128], BF16), ("negmask", [128], BF16), ("mhalf", [1], F32),
]


def build_program(layers, first, final):
    nc = bass.Bass("TRN2", target_bir_lowering=False)
    _, CD = make_consts()

    def din(name, shape, dt=F32):
        return nc.dram_tensor(name, list(shape), dt, kind="ExternalInput").ap()

    x_d = din("x", [S, D])
    n1_d = din("norm1_gT", [128, DEPTH * 8])
    n2_d = din("norm2_gT", [128, DEPTH * 8])
    rg_d = din("ret_gT", [128, DEPTH * NH])
    sg_d = din("sb_gT", [128, DEPTH * NH])
    fg_d = din("final_gT", [128, 8])
    w_in_d = din("w_in", [DEPTH, D, INW])
    w_out_d = din("w_out", [DEPTH, D, D])
    w_gate_d = din("w_gate", [DEPTH, D, DFF])
    w_up_d = din("w_up", [DEPTH, D, DFF])
    w_down_d = din("w_down", [DEPTH, DFF, D])
    cdram = {}
    for name, free, dt in CONST_SPECS:
        cdram[name] = din("c_" + name, [128] + free, dt)
    y_d = nc.dram_tensor("y", [S, D], F32, kind="ExternalOutput").ap()

    A = Arena(nc, ARENA_BYTES)
    p = Prog(nc)
    ps = nc.alloc_psum_tensor("ps", [128, 8, 512], F32)
    Tps = [T("ps%d" % b) for b in range(8)]

    def psb(b):
        return ps[:, b, :]

    def psb16(b):
        return ps[:, b, :].bitcast(BF16)

    xT = A.alloc([8, S], F32)
    hT = A.alloc([8, S], BF16)
    Tx = [[T("x%d_%d" % (c, n)) for n in range(NT)] for c in range(8)]
    Th = [[T("h%d_%d" % (c, n)) for n in range(NT)] for c in range(8)]
    ring = [A.alloc([1024], BF16) for _ in range(RING)]
    Tring = [T("ring%d" % i) for i in range(RING)]
    C = {}
    TC = T("consts")
    for name, free, dt in CONST_SPECS:
        C[name] = A.alloc(free, dt)
    gvec = {}
    gsrc = (("n1", n1_d, DEPTH * 8), ("n2", n2_d, DEPTH * 8), ("rg", rg_d, DEPTH * NH), ("sg", sg_d, DEPTH * NH), ("fg", fg_d, 8))
    for name, src, n in gsrc:
        gvec[name] = A.alloc([n], F32)
    ins = [I("dma_start", out=C[name], in_=cdram[name]) for name, _, _ in CONST_SPECS]
    ins += [I("dma_start", out=gvec[name], in_=src) for name, src, _ in gsrc]
    p.op("sp", ins, writes=[TC], dma=len(ins))

    wq = []
    wstate = {"next_dma": 0, "next_use": 0}
    Txload = T("xload")

    def cn(src2d):
        return src2d.rearrange("(c p) n -> p c n", p=128)

    for l in layers:
        for h in range(NH):
            for j in range(4):
                c0 = j * 512 + h * 128
                wq.append(("cn", cn(w_in_d[l, :, c0:c0 + 128])))
        for hh in range(4):
            wq.append(("n", w_out_d[l, hh * 128:(hh + 1) * 128, :]))
        for h in range(NH):
            for j in range(3):
                c0 = 2048 + j * 512 + h * 128
                wq.append(("cn", cn(w_in_d[l, :, c0:c0 + 128])))
        for hh in range(4):
            wq.append(("n", w_out_d[l, 512 + hh * 128:512 + (hh + 1) * 128, :]))
        for g in range(NFC // FG):
            for f in range(g * FG, (g + 1) * FG):
                wq.append(("cn", cn(w_gate_d[l, :, f * 128:(f + 1) * 128])))
                wq.append(("cn", cn(w_up_d[l, :, f * 128:(f + 1) * 128])))
            for f in range(g * FG, (g + 1) * FG):
                wq.append(("n", w_down_d[l, f * 128:(f + 1) * 128, :]))

    def ring_view(slot, kind):
        ap = ring[slot]
        if kind == "cn":
            ap = ap.rearrange("p (c n) -> p c n", c=8)
        return ap

    def weight_done():
        k = wstate["next_dma"]
        if k >= len(wq):
            return
        kind, src = wq[k]
        slot = k % RING
        p.op("pool", [I("dma_start", out=ring_view(slot, kind), in_=src)], reads=([Txload] if k < RING else []), writes=[Tring[slot]], dma=1, nobar=True)
        wstate["next_dma"] = k + 1

    def next_weight(kind):
        k = wstate["next_use"]
        assert wq[k][0] == kind, (k, wq[k][0], kind)
        assert k < wstate["next_dma"]
        wstate["next_use"] = k + 1
        slot = k % RING
        return ring_view(slot, kind), Tring[slot]

    base_mark = A.mark()

    def phase_load_x():
        m = A.mark()
        xin = [A.alloc([D], F32) for _ in range(6)]
        Txin = [T("xin%d" % i) for i in range(6)]
        for b in range(NB):
            xi, txi = xin[b % 6], Txin[b % 6]
            p.op("sp", [I("dma_start", out=xi, in_=x_d[b * 128:(b + 1) * 128, :])], writes=[txi] + ([Txload] if b == NB - 3 else []), dma=1)
            for half in range(2):
                bank = (2 * b + half) % 8
                ins = [I("transpose", out=psb(bank)[:, j * 128:(j + 1) * 128],
                         in_=xi[:, (half * 4 + j) * 128:(half * 4 + j + 1) * 128], identity=C["ident_f"]) for j in range(4)]
                p.op("pe", ins, reads=[txi, TC], writes=[Tps[bank]])
                dst = xT[:, half * 4:half * 4 + 4, b * 128:(b + 1) * 128]
                src = psb(bank).rearrange("p (j t) -> p j t", j=4)
                wr = [Tx[c][b // 4] for c in range(half * 4, half * 4 + 4)]
                if half == 0:
                    p.op("act", [I("copy", out=dst, in_=src)], reads=[Tps[bank]], writes=wr)
                else:
                    p.op("dve", [I("tensor_copy", out=dst, in_=src)], reads=[Tps[bank]], writes=wr)
            if b == NB - 3:
                for _ in range(RING):
                    weight_done()
        p.barrier()
        A.release(m)

    def norm_scratch():
        sq = [A.alloc([TW], BF16) for _ in range(2)]
        rs = [A.alloc([TW], F32) for _ in range(2)]
        return sq, [T("sq0"), T("sq1")], rs, [T("rs0"), T("rs1")]

    def rmsnorm_tile(n, dst_fn, gcols, bank, scratch):
        sq, Tsq, rs, Trs = scratch
        tsl = slice(n * TW, (n + 1) * TW)
        for c in range(8):
            k = c % 2
            p.op("act", [I("activation", out=sq[k], in_=xT[:, c, tsl], func=AF.Square)], reads=[Tx[c][n]], writes=[Tsq[k]])
            p.op("pe", [I("matmul", psb(bank), lhsT=C["onesD"], rhs=sq[k], start=(c == 0), stop=(c == 7))],
                 reads=[Tsq[k], TC], writes=[Tps[bank]])
        r, tr_ = rs[n % 2], Trs[n % 2]
        p.op("act", [I("activation", out=r, in_=psb(bank), func=AF.Ln, bias=EPS)], reads=[Tps[bank]], writes=[tr_])
        p.op("act", [I("activation", out=r, in_=r, func=AF.Exp, scale=-0.5)], reads=[tr_], writes=[tr_])
        for c in range(8):
            d_, td_ = dst_fn(c)
            p.op("dve", [I("scalar_tensor_tensor", out=d_, in0=xT[:, c, tsl], scalar=gcols[:, c:c + 1], in1=r, op0=ALU.mult, op1=ALU.mult)],
                 reads=[Tx[c][n], tr_, TC], writes=[td_])

    def rmsnorm_h(gcols):
        m = A.mark()
        sc = norm_scratch()
        for n in range(NT):
            tsl = slice(n * TW, (n + 1) * TW)
            rmsnorm_tile(n, lambda c, n=n, tsl=tsl: (hT[:, c, tsl], Th[c][n]), gcols, n % 2, sc)
        p.barrier()
        A.release(m)

    def wout_half(catT, Tcat):
        ws = [next_weight("n") for _ in range(4)]
        i = 0
        for n in range(NT):
            for c in range(8):
                bank = i % 4
                i += 1
                tsl = slice(n * TW, (n + 1) * TW)
                ins = [I("matmul", psb(bank), lhsT=ws[hh][0][:, c * 128:(c + 1) * 128], rhs=catT[:, hh, tsl],
                         start=(hh == 0), stop=(hh == 3)) for hh in range(4)]
                p.op("pe", ins, reads=[w[1] for w in ws] + [Tcat[hh][n] for hh in range(4)], writes=[Tps[bank]])
                p.op("dve", [I("tensor_tensor", out=xT[:, c, tsl], in0=psb(bank), in1=xT[:, c, tsl], op=ALU.add)],
                     reads=[Tps[bank], Tx[c][n]], writes=[Tx[c][n]])
        for _ in range(4):
            weight_done()

    def phase_retention(l, catT, Tcat):
        m = A.mark()
        qkT = A.alloc([2, S], BF16)
        qdT = A.alloc([S], BF16)
        ktok = A.alloc([NB, 128], BF16); v = A.alloc([NB, 128], BF16); vd = A.alloc([NB, 128], BF16)
        gateT = A.alloc([S], BF16)
        Sbf = A.alloc([NB, 128], BF16)
        Sf = A.alloc([128], F32)
        tA = [A.alloc([2, 128], F32) for _ in range(2)]
        tB = [A.alloc([2, 128], F32) for _ in range(2)]
        qrot = [A.alloc([128], BF16) for _ in range(2)]
        smT = [A.alloc([4, 128], BF16) for _ in range(2)]
        obf = [A.alloc([TW], BF16) for _ in range(2)]
        osq = [A.alloc([TW], BF16) for _ in range(2)]
        rr = [A.alloc([TW], F32) for _ in range(2)]
        Tqk, TqdT, Tktok, Tv, Tvd, Tgate, TSbf, TSf = [T(n) for n in "qk qdT ktok v vd gate Sbf Sf".split()]
        TtA = [T("tA0"), T("tA1")]; TtB = [T("tB0"), T("tB1")]; Tqrot = [T("qr0"), T("qr1")]
        TsmT = [T("sm0"), T("sm1")]
        Tobf = [T("ob0"), T("ob1")]; Tosq = [T("os0"), T("os1")]; Trr = [T("rr0"), T("rr1")]
        qT = qkT[:, 0, :]
        kT = qkT[:, 1, :]

        for h in range(NH):
            (wq_, Twq), (wk_, Twk), (wv_, Twv), (wg_, Twg) = [next_weight("cn") for _ in range(4)]

            def qkv(b):
                bank = 2 + (b % 2)
                bsl = slice(b * 128, (b + 1) * 128)
                ins = []
                if (wk_.offset - wq_.offset == 1024) and (wv_.offset - wk_.offset == 1024):
                    out3 = psb(bank)[:, 0:384].rearrange("p (j n) -> p j n", j=3)
                    for c in range(8):
                        ins.append(I("matmul", out3, lhsT=hT[:, c, bsl], rhs=mkap(wq_[:, c, :], 0, [[1024, 3], [1, 128]]),
                                     start=(c == 0), stop=(c == 7)))
                else:
                    for j, w_ in enumerate((wq_, wk_, wv_)):
                        for c in range(8):
                            ins.append(I("matmul", psb(bank)[:, j * 128:(j + 1) * 128], lhsT=hT[:, c, bsl], rhs=w_[:, c, :],
                                         start=(c == 0), stop=(c == 7)))
                p.op("pe", ins, reads=[Twq, Twk, Twv] + [Th[c][b // 4] for c in range(8)], writes=[Tps[bank]])

            def gate_tile(n):
                bank = n % 2
                tsl = slice(n * TW, (n + 1) * TW)
                ins = [I("matmul", psb(bank), lhsT=wg_[:, c, :], rhs=hT[:, c, tsl], start=(c == 0), stop=(c == 7)) for c in range(8)]
                p.op("pe", ins, reads=[Twg] + [Th[c][n] for c in range(8)], writes=[Tps[bank]])
                p.op("act", [I("activation", out=gateT[:, tsl], in_=psb(bank), func=AF.Silu)], reads=[Tps[bank]], writes=[Tgate])

            def evac(b):
                tbank = 4 + (b % 2)
                bsl = slice(b * 128, (b + 1) * 128)
                p.op("dve", [I("tensor_copy", out=qkT[:, :, bsl], in_=psb16(tbank)[:, 0:256].rearrange("p (j t) -> p j t", j=2))],
                     reads=[Tps[tbank]], writes=[Tqk])
                p.op("dve", [I("tensor_tensor", out=qdT[:, bsl], in0=psb16(tbank)[:, 0:128], in1=C["qdec"][:, h, :], op=ALU.mult)],
                     reads=[Tps[tbank], TC], writes=[TqdT])

            p.op("pool", [I("memset", Sf, 0.0)], writes=[TSf])
            p.op("pool", [I("memset", Sbf[:, 0, :], 0.0)], writes=[TSbf])
            qkv(0)
            for b in range(NB):
                if b + 1 < NB:
                    qkv(b + 1)
                bank = 2 + (b % 2)
                a_, ta_ = tA[b % 2], TtA[b % 2]
                b_, tb_ = tB[b % 2], TtB[b % 2]
                xq = psb(bank)[:, 0:256]
                x4 = mkap(xq, 0, [[128, 2], [64, 2], [1, 64]])
                xsw = mkap(xq, 64, [[128, 2], [-64, 2], [1, 64]])
                cos4 = mkap(C["cos"][:, b, :], 0, [[0, 2], [0, 2], [1, 64]])
                sin4 = mkap(C["sins"][:, b, :, :], 0, [[0, 2], [64, 2], [1, 64]])
                a4 = a_.rearrange("p q (a b) -> p q a b", a=2)
                b4 = b_.rearrange("p q (a b) -> p q a b", a=2)
                p.op("dve", [I("tensor_tensor", out=a4, in0=x4, in1=cos4, op=ALU.mult)], reads=[Tps[bank], TC], writes=[ta_])
                p.op("dve", [I("tensor_tensor", out=b4, in0=xsw, in1=sin4, op=ALU.mult)], reads=[Tps[bank], TC], writes=[tb_])
                qr, tqr = qrot[b % 2], Tqrot[b % 2]
                p.op("pool", [I("tensor_tensor", out=qr, in0=a_[:, 0, :], in1=b_[:, 0, :], op=ALU.add)], reads=[ta_, tb_], writes=[tqr])
                p.op("pool", [I("tensor_tensor", out=ktok[:, b, :], in0=a_[:, 1, :], in1=b_[:, 1, :], op=ALU.add)], reads=[ta_, tb_], writes=[Tktok])
                p.op("dve", [I("tensor_copy", out=v[:, b, :], in_=psb(bank)[:, 256:384])], reads=[Tps[bank]], writes=[Tv])
                p.op("dve", [I("tensor_scalar_mul", out=vd[:, b, :], in0=psb(bank)[:, 256:384], scalar1=C["kdec"][:, h:h + 1])],
                     reads=[Tps[bank], TC], writes=[Tvd])
                tbank = 4 + (b % 2)
                ins = [I("transpose", out=psb16(tbank)[:, 0:128], in_=qr, identity=C["ident_bf"]),
                       I("transpose", out=psb16(tbank)[:, 128:256], in_=ktok[:, b, :], identity=C["ident_bf"])]
                p.op("pe", ins, reads=[tqr, Tktok, TC], writes=[Tps[tbank]])
                if b >= 1:
                    evac(b - 1)
                    pbank = 6 + ((b - 1) % 2)
                    p.op("dve", [I("scalar_tensor_tensor", out=Sf, in0=Sf, scalar=CD[h], in1=psb(pbank)[:, 0:128], op0=ALU.mult, op1=ALU.add)],
                         reads=[TSf, Tps[pbank]], writes=[TSf])
                    p.op("pool", [I("tensor_copy", out=Sbf[:, b, :], in_=Sf)], reads=[TSf], writes=[TSbf])
                if b < NB - 1:
                    ubank = 6 + (b % 2)
                    p.op("pe", [I("matmul", psb(ubank)[:, 0:128], lhsT=ktok[:, b, :], rhs=vd[:, b, :], start=True, stop=True)],
                         reads=[Tktok, Tvd], writes=[Tps[ubank]])
                if b % 4 == 2:
                    gate_tile(b // 4)
            evac(NB - 1)
            for _ in range(4):
                weight_done()
            def tile_s1(n):
                sbank = n % 2
                obank = 2 + (n % 2)
                ins = []
                for q in range(4):
                    bsl = slice((4 * n + q) * 128, (4 * n + q + 1) * 128)
                    ins.append(I("matmul", psb(sbank)[:, q * 128:(q + 1) * 128], lhsT=kT[:, bsl], rhs=qT[:, bsl], start=True, stop=True))
                p.op("pe", ins, reads=[Tqk], writes=[Tps[sbank]])
                sm, tsm = smT[n % 2], TsmT[n % 2]
                mk3 = mkap(C["maskT"][:, h, :], 0, [[0, 4], [1, 128]])
                p.op("dve", [I("tensor_tensor", out=sm, in0=psb(sbank).rearrange("p (q i) -> p q i", q=4), in1=mk3, op=ALU.mult)],
                     reads=[Tps[sbank], TC], writes=[tsm])

            def tile_s1b(n):
                obank = 2 + (n % 2)
                sm, tsm = smT[n % 2], TsmT[n % 2]
                ins = []
                for q in range(4):
                    blk = 4 * n + q
                    bsl = slice(blk * 128, (blk + 1) * 128)
                    ins.append(I("matmul", psb(obank)[:, q * 128:(q + 1) * 128], lhsT=v[:, blk, :], rhs=sm[:, q, :], start=True, stop=False))
                    ins.append(I("matmul", psb(obank)[:, q * 128:(q + 1) * 128], lhsT=Sbf[:, blk, :], rhs=qdT[:, bsl], start=False, stop=True))
                p.op("pe", ins, reads=[Tv, tsm, TSbf, TqdT], writes=[Tps[obank]])
                p.op("act", [I("activation", out=obf[n % 2], in_=psb(obank), func=AF.Copy, scale=SCALE)], reads=[Tps[obank]], writes=[Tobf[n % 2]])

            def tile_s2(n):
                cbank = 4 + (n % 2)
                p.op("pe", [I("matmul", psb(cbank), lhsT=C["cenM"], rhs=obf[n % 2], start=True, stop=True)], reads=[Tobf[n % 2], TC], writes=[Tps[cbank]])
                p.op("act", [I("activation", out=osq[n % 2], in_=psb(cbank), func=AF.Square)], reads=[Tps[cbank]], writes=[Tosq[n % 2]])

            def tile_s3(n):
                tsl = slice(n * TW, (n + 1) * TW)
                cbank = 4 + (n % 2)
                vbank = 6 + (n % 2)
                r_, tr_ = rr[n % 2], Trr[n % 2]
                p.op("pe", [I("matmul", psb(vbank), lhsT=C["onesH"], rhs=osq[n % 2], start=True, stop=True)], reads=[Tosq[n % 2], TC], writes=[Tps[vbank]])
                p.op("act", [I("activation", out=r_, in_=psb(vbank), func=AF.Ln, bias=EPS)], reads=[Tps[vbank]], writes=[tr_])
                p.op("act", [I("activation", out=r_, in_=r_, func=AF.Exp, scale=-0.5)], reads=[tr_], writes=[tr_])
                p.op("pool", [I("tensor_tensor", out=r_, in0=r_, in1=gateT[:, tsl], op=ALU.mult)], reads=[tr_, Tgate], writes=[tr_])
                gcol = gvec["rg"][:, l * NH + h:l * NH + h + 1]
                p.op("dve", [I("scalar_tensor_tensor", out=catT[:, h, tsl], in0=psb(cbank), scalar=gcol, in1=r_, op0=ALU.mult, op1=ALU.mult)],
                     reads=[Tps[cbank], tr_, TC], writes=[Tcat[h][n]])

            for k in range(NT + 3):
                if k < NT:
                    tile_s1(k)
                if 0 <= k - 1 < NT:
                    tile_s1b(k - 1)
                if 0 <= k - 2 < NT:
                    tile_s2(k - 2)
                if 0 <= k - 3 < NT:
                    tile_s3(k - 3)
        p.barrier()
        A.release(m)

    def phase_sb(l, catT, Tcat):
        m = A.mark()
        NE, NW = 3, 3
        qTb = [A.alloc([S], BF16) for _ in range(2)]
        kTb = [A.alloc([S], BF16) for _ in range(2)]
        vb = [A.alloc([NB, 128], BF16) for _ in range(2)]
        TqT = [T("sqT0"), T("sqT1")]; TkT = [T("skT0"), T("skT1")]; Tv = [T("sv0"), T("sv1")]
        Gb = [A.alloc([520], F32) for _ in range(NE)]
        Qb = [A.alloc([520], F32) for _ in range(NE)]
        wb = [A.alloc([512], BF16) for _ in range(NW)]
        wTb = [A.alloc([4, 128], BF16) for _ in range(2)]
        carry = [A.alloc([1], F32) for _ in range(2)]
        osq = [A.alloc([TW], BF16) for _ in range(2)]
        rs = [A.alloc([TW], F32) for _ in range(2)]
        TG = [T("G%d" % i) for i in range(NE)]; TQ = [T("Q%d" % i) for i in range(NE)]
        for i in range(NE):
            p.op("pool", [I("memset", Gb[i][:, 512:513], 1.0)], writes=[TG[i]])
        Tw = [T("w%d" % i) for i in range(NW)]; TwT = [T("wT0"), T("wT1")]
        Tcarry = [T("cy0"), T("cy1")]; Tosq = [T("sos0"), T("sos1")]; Trs = [T("srs0"), T("srs1")]
        ZB = [0, 1, 2]
        PB = 3
        TB = [4, 5]
        OB = [6, 7]

        def proj_pieces(h):
            par = h % 2
            ws = [next_weight("cn") for _ in range(3)]
            (wq_, Twq), (wk_, Twk), (wv_, Twv) = ws
            pieces = []
            for j, (w_, tw_, dst, tdst) in enumerate(((wq_, Twq, qTb[par], TqT[par]), (wk_, Twk, kTb[par], TkT[par]))):
                for n in range(NT):
                    def piece(j=j, w_=w_, tw_=tw_, dst=dst, tdst=tdst, n=n):
                        tsl = slice(n * TW, (n + 1) * TW)
                        ins = [I("matmul", psb(PB), lhsT=w_[:, c, :], rhs=hT[:, c, tsl], start=(c == 0), stop=(c == 7)) for c in range(8)]
                        p.op("pe", ins, reads=[tw_] + [Th[c][n] for c in range(8)], writes=[Tps[PB]])
                        if j == 0:
                            p.op("act", [I("activation", out=dst[:, tsl], in_=psb(PB), func=AF.Copy, scale=SCALE)], reads=[Tps[PB]], writes=[tdst])
                        else:
                            p.op("act", [I("copy", out=dst[:, tsl], in_=psb(PB))], reads=[Tps[PB]], writes=[tdst])
                    pieces.append(piece)
            for g4 in range(4):
                def piece(g4=g4, wv_=wv_, Twv=Twv, par=par):
                    ins = []
                    for q in range(4):
                        bsl = slice((4 * g4 + q) * 128, (4 * g4 + q + 1) * 128)
                        for c in range(8):
                            ins.append(I("matmul", psb(PB)[:, q * 128:(q + 1) * 128], lhsT=hT[:, c, bsl], rhs=wv_[:, c, :], start=(c == 0), stop=(c == 7)))
                    p.op("pe", ins, reads=[Twv] + [Th[c][g4] for c in range(8)], writes=[Tps[PB]])
                    p.op("act", [I("copy", out=vb[par][:, 4 * g4:4 * g4 + 4, :], in_=psb(PB).rearrange("p (q e) -> p q e", q=4))],
                         reads=[Tps[PB]], writes=[Tv[par]])
                pieces.append(piece)

            def done():
                for _ in range(3):
                    weight_done()
            pieces.append(done)
            return pieces

        units = []
        head_start = []
        for h in range(NH):
            head_start.append(len(units))
            for Tq in range(NB):
                nk = 128 * (Tq + 1)
                hi = nk
                first = True
                while hi > 0:
                    lo = max(0, hi - 512)
                    units.append((h, Tq, lo, hi, first, lo == 0))
                    first = False
                    hi = lo
        U = len(units)

        def st_z(u):
            h, Tq, k0, k1, first, last = units[u]
            par = h % 2
            n_ = k1 - k0
            zb = ZB[u % 3]
            G, tG = Gb[u % NE], TG[u % NE]
            qsl = slice(Tq * 128, (Tq + 1) * 128)
            ins = [I("matmul", psb(zb)[:, 0:n_], lhsT=qTb[par][:, qsl], rhs=kTb[par][:, k0:k1], start=True, stop=not first)]
            if first:
                ins.append(I("matmul", psb(zb)[:, n_ - 128:n_], lhsT=C["ident_bf"], rhs=C["negmask"], start=False, stop=True))
            p.op("pe", ins, reads=[TqT[par], TkT[par], TC], writes=[Tps[zb]])
            p.op("act", [I("activation", out=G[:, 512 - n_:512], in_=psb(zb)[:, 0:n_], func=AF.Sigmoid, scale=-1.0)], reads=[Tps[zb]], writes=[tG])

        def st_scan(u):
            h, Tq, k0, k1, first, last = units[u]
            n_ = k1 - k0
            G, tG = Gb[u % NE], TG[u % NE]
            Q, tQ = Qb[u % NE], TQ[u % NE]
            gr = mkap(G, 512, [[-1, n_ + 1]])
            qr = mkap(Q, 512, [[-1, n_ + 1]])
            onr = mkap(C["ones_col"], 0, [[0, n_ + 1]])
            if first:
                p.op("dve", [I("tensor_tensor_scan", out=qr, data0=gr, data1=onr, initial=1.0, op0=ALU.mult, op1=ALU.mult)],
                     reads=[tG, TC], writes=[tQ])
            else:
                pu = u - 1
                pn = units[pu][3] - units[pu][2]
                cy = Qb[pu % NE][:, 512 - pn:513 - pn]
                p.op("dve", [I("tensor_tensor_scan", out=qr, data0=gr, data1=onr, initial=cy, op0=ALU.mult, op1=ALU.mult)],
                     reads=[tG, TC, TQ[pu % NE]], writes=[tQ])

        def st_w(u):
            h, Tq, k0, k1, first, last = units[u]
            n_ = k1 - k0
            Q, tQ = Qb[u % NE], TQ[u % NE]
            w_, tw_ = wb[u % NW], Tw[u % NW]
            p.op("pool", [I("tensor_tensor", out=w_[:, 0:n_], in0=Q[:, 513 - n_:513], in1=Q[:, 512 - n_:512], op=ALU.subtract)], reads=[tQ], writes=[tw_])

        def st_tr(u):
            h, Tq, k0, k1, first, last = units[u]
            nkb = (k1 - k0) // 128
            w_, tw_ = wb[u % NW], Tw[u % NW]
            wT, twT = wTb[u % 2], TwT[u % 2]
            tb = TB[u % 2]
            ins = [I("transpose", out=psb16(tb)[:, i * 128:(i + 1) * 128], in_=w_[:, i * 128:(i + 1) * 128], identity=C["ident_bf"]) for i in range(nkb)]
            p.op("pe", ins, reads=[tw_, TC], writes=[Tps[tb]])
            p.op("act", [I("copy", out=wT[:, 0:nkb, :], in_=psb16(tb)[:, 0:nkb * 128].rearrange("p (i t) -> p i t", i=nkb))],
                 reads=[Tps[tb]], writes=[twT])

        def st_av(u):
            h, Tq, k0, k1, first, last = units[u]
            par = h % 2
            nkb = (k1 - k0) // 128
            wT, twT = wTb[u % 2], TwT[u % 2]
            ob = OB[(Tq // 4) % 2]
            q = Tq % 4
            ins = [I("matmul", psb(ob)[:, q * 128:(q + 1) * 128], lhsT=vb[par][:, k0 // 128 + i, :], rhs=wT[:, i, :],
                     start=(first and i == 0), stop=(last and i == nkb - 1)) for i in range(nkb)]
            p.op("pe", ins, reads=[Tv[par], twT], writes=[Tps[ob]])
            if last and q == 3:
                n = Tq // 4
                tsl = slice(n * TW, (n + 1) * TW)
                oq, toq = osq[n % 2], Tosq[n % 2]
                r_, tr_ = rs[n % 2], Trs[n % 2]
                mbank = PB
                gcol = gvec["sg"][:, l * NH + h:l * NH + h + 1]

                def n1():
                    p.op("act", [I("activation", out=oq, in_=psb(ob), func=AF.Square)], reads=[Tps[ob]], writes=[toq])

                def n2():
                    p.op("pe", [I("matmul", psb(mbank), lhsT=C["onesH"], rhs=oq, start=True, stop=True)], reads=[toq, TC], writes=[Tps[mbank]])

                def n3():
                    p.op("act", [I("activation", out=r_, in_=psb(mbank), func=AF.Ln, bias=EPS)], reads=[Tps[mbank]], writes=[tr_])
                    p.op("act", [I("activation", out=r_, in_=r_, func=AF.Exp, scale=-0.5)], reads=[tr_], writes=[tr_])

                def n4():
                    p.op("dve", [I("scalar_tensor_tensor", out=catT[:, h, tsl], in0=psb(ob), scalar=gcol, in1=r_, op0=ALU.mult, op1=ALU.mult)],
                         reads=[Tps[ob], tr_, TC], writes=[Tcat[h][n]])
                deferred.setdefault(cur_it[0] + 1, []).append(n1)
                deferred.setdefault(cur_it[0] + 2, []).append(n2)
                deferred_end.setdefault(cur_it[0] + 2, []).append(n3)
                deferred.setdefault(cur_it[0] + 3, []).append(n4)

        for pc in proj_pieces(0):
            pc()
        pending = []
        deferred = {}
        deferred_end = {}
        cur_it = [0]
        for i in range(U + 10):
            cur_it[0] = i
            for f in deferred.pop(i, []):
                f()
            if 0 <= i - 4 < U:
                st_tr(i - 4)
            if 0 <= i - 5 < U:
                st_av(i - 5)
            if i < U:
                h = units[i][0]
                if i == head_start[h] and h + 1 < NH:
                    pending = proj_pieces(h + 1)
                st_z(i)
            if 0 <= i - 2 < U:
                st_w(i - 2)
            if 0 <= i - 1 < U:
                st_scan(i - 1)
            for f in deferred_end.pop(i, []):
                f()
            if pending and i < U and (i - head_start[units[i][0]]) % 3 == 2:
                pending.pop(0)()
        assert not pending and not deferred and not deferred_end
        p.barrier()
        A.release(m)

    def phase_ffn(l):
        m = A.mark()
        actT = A.alloc([FG, S], BF16)
        Tact = [[T("a%d_%d" % (f, n)) for n in range(NT)] for f in range(FG)]
        sg = [A.alloc([TW], BF16) for _ in range(2)]
        Tsg = [T("sg0"), T("sg1")]
        i = 0
        for g in range(NFC // FG):
            for fl in range(FG):
                (wg_, Twg), (wu_, Twu) = next_weight("cn"), next_weight("cn")
                for n in range(NT):
                    tsl = slice(n * TW, (n + 1) * TW)
                    gb = (2 * i) % 8
                    ub = (2 * i + 1) % 8
                    i += 1
                    for (w_, tw_, bank) in ((wg_, Twg, gb), (wu_, Twu, ub)):
                        ins = [I("matmul", psb(bank), lhsT=w_[:, c, :], rhs=hT[:, c, tsl], start=(c == 0), stop=(c == 7)) for c in range(8)]
                        p.op("pe", ins, reads=[tw_] + [Th[c][n] for c in range(8)], writes=[Tps[bank]])
                    s_, ts_ = sg[i % 2], Tsg[i % 2]
                    p.op("act", [I("activation", out=s_, in_=psb(gb), func=AF.Silu)], reads=[Tps[gb]], writes=[ts_])
                    p.op("dve", [I("tensor_tensor", out=actT[:, fl, tsl], in0=psb(ub), in1=s_, op=ALU.mult)], reads=[Tps[ub], ts_], writes=[Tact[fl][n]])
                weight_done()
                weight_done()
            wds = [next_weight("n") for _ in range(FG)]
            for c in range(8):
                for n in range(NT):
                    tsl = slice(n * TW, (n + 1) * TW)
                    bank = i % 8
                    i += 1
                    ins = [I("matmul", psb(bank), lhsT=wds[fl][0][:, c * 128:(c + 1) * 128], rhs=actT[:, fl, tsl], start=(fl == 0), stop=(fl == FG - 1))
                           for fl in range(FG)]
                    p.op("pe", ins, reads=[w[1] for w in wds] + [Tact[fl][n] for fl in range(FG)], writes=[Tps[bank]])
                    p.op("dve", [I("tensor_tensor", out=xT[:, c, tsl], in0=psb(bank), in1=xT[:, c, tsl], op=ALU.add)],
                         reads=[Tps[bank], Tx[c][n]], writes=[Tx[c][n]])
            for _ in range(FG):
                weight_done()
        p.barrier()
        A.release(m)

    def phase_out(do_norm):
        m = A.mark()
        sc = norm_scratch()
        yt = [A.alloc([8, TW], F32) for _ in range(2)]
        Tyt = [[T("yt%d_%d" % (k, c)) for c in range(8)] for k in range(2)]
        yo = [A.alloc([D], F32) for _ in range(4)]
        Tyo = [T("yo%d" % i) for i in range(4)]
        Tout = T("out")

        def norm(n):
            k = n % 2
            rmsnorm_tile(n, lambda c, k=k: (yt[k][:, c, :], Tyt[k][c]), gvec["fg"], n % 2, sc)

        def trans(n):
            for bq in range(4):
                b = 4 * n + bq
                y_, ty_ = yo[b % 4], Tyo[b % 4]
                for half in range(2):
                    bank = 2 + (2 * b + half) % 6
                    ins = []
                    rd = [TC]
                    for j in range(4):
                        c = half * 4 + j
                        if do_norm:
                            src = yt[n % 2][:, c, bq * 128:(bq + 1) * 128]
                            rd.append(Tyt[n % 2][c])
                        else:
                            src = xT[:, c, b * 128:(b + 1) * 128]
                            rd.append(Tx[c][n])
                        ins.append(I("transpose", out=psb(bank)[:, j * 128:(j + 1) * 128], in_=src, identity=C["ident_f"]))
                    p.op("pe", ins, reads=rd, writes=[Tps[bank]])
                    if half == 0:
                        p.op("act", [I("copy", out=y_[:, 0:512], in_=psb(bank))], reads=[Tps[bank]], writes=[ty_])
                    else:
                        p.op("dve", [I("tensor_copy", out=y_[:, 512:1024], in_=psb(bank))], reads=[Tps[bank]], writes=[ty_])
                p.op("sp", [I("dma_start", out=y_d[b * 128:(b + 1) * 128, :], in_=y_)], reads=[ty_], writes=[Tout], dma=1)

        if do_norm:
            norm(0)
        for n in range(NT):
            if do_norm and n + 1 < NT:
                norm(n + 1)
            trans(n)
        p.op("sp", None, reads=[Tout])
        A.release(m)

    import os
    KSTOP = os.environ.get("KSTOP", "")
    phase_load_x()
    stopped = False
    for l in layers:
        m = A.mark()
        catT = A.alloc([4, S], BF16)
        Tcat = [[T("cat%d_%d" % (hh, n)) for n in range(NT)] for hh in range(4)]
        rmsnorm_h(gvec["n1"][:, l * 8:(l + 1) * 8])
        if KSTOP == "norm1":
            stopped = True
            break
        phase_retention(l, catT, Tcat)
        if KSTOP == "ret":
            stopped = True
            break
        wout_half(catT, Tcat)
        if KSTOP == "wout1":
            stopped = True
            break
        phase_sb(l, catT, Tcat)
        if KSTOP == "sb":
            stopped = True
            break
        wout_half(catT, Tcat)
        if KSTOP == "wout2":
            p.barrier()
            stopped = True
            break
        rmsnorm_h(gvec["n2"][:, l * 8:(l + 1) * 8])
        A.release(m)
        phase_ffn(l)
    if stopped:
        A.release(base_mark)
    phase_out(final and not stopped)
    if not stopped:
        assert wstate["next_use"] == len(wq), (wstate, len(wq))
    print("SBUF arena peak bytes/partition:", A.peak, "ops:", {e: len(p.ops[e]) for e in p.ENG})
    p.emit()
    return nc


_PROG_CACHE = {}


def _get_prog(key):
    if key not in _PROG_CACHE:
        _PROG_CACHE[key] = build_program(*key)
    return _PROG_CACHE[key]


def _colsT(g, n):
    g = np.asarray(g, dtype=np.float32).reshape(-1, n, 128)
    return np.ascontiguousarray(g.transpose(2, 0, 1).reshape(128, -1))


FUSED = True


def kernel(x, norm1_g, w_in, ret_norm_g, sb_norm_g, w_out, norm2_g, w_gate, w_up, w_down, final_g):
    consts, _ = make_consts()
    shared = {
        "norm1_gT": _colsT(norm1_g, 8), "norm2_gT": _colsT(norm2_g, 8),
        "ret_gT": _colsT(ret_norm_g, NH), "sb_gT": _colsT(sb_norm_g, NH),
        "final_gT": _colsT(final_g, 8),
        "w_in": np.ascontiguousarray(w_in, dtype=np.float32), "w_out": np.ascontiguousarray(w_out, dtype=np.float32),
        "w_gate": np.ascontiguousarray(w_gate, dtype=np.float32), "w_up": np.ascontiguousarray(w_up, dtype=np.float32),
        "w_down": np.ascontiguousarray(w_down, dtype=np.float32),
    }
    for name, _, _ in CONST_SPECS:
        shared["c_" + name] = consts[name]
    xs = [np.ascontiguousarray(x[i], dtype=np.float32) for i in range(NCORES)]
    if FUSED:
        stages = [((0, 1), True, True)]
    else:
        stages = [((0,), True, False), ((1,), True, True)]
    for (layers, first, final) in stages:
        nc = _get_prog((tuple(layers), first, final))
        in_maps = [dict(shared, x=xs[i]) for i in range(NCORES)]
        res = run_bass_kernel_spmd(nc, in_maps, core_ids=list(range(NCORES)))
        xs = [np.asarray(res.results[i]["y"], dtype=np.float32) for i in range(NCORES)]
    return np.stack(xs, axis=0).astype(np.float32)
```

```python
import contextlib
import numpy as np
import ml_dtypes
import concourse.bass as bass
import concourse.mybir as mybir
from concourse.ap import AP
from concourse.bass_utils import run_bass_kernel_spmd

F32 = mybir.dt.float32
BF16 = mybir.dt.bfloat16
ALU = mybir.AluOpType
AF = mybir.ActivationFunctionType

D = 1024
S = 2048
DEPTH = 2
HD = 128
NH = 4
DFF = 2816
NFC = DFF // 128
NB = S // 128
NT = 4
TW = 512
INW = 3584
EPS = 1e-6
NCORES = 8
RING = 14
ARENA_BYTES = 212480
FG = 11
SCALE = float(HD ** -0.5)


class T:
    __slots__ = ("name", "w", "rc", "rd")

    def __init__(self, name=""):
        self.name = name
        self.w = None
        self.rc = {}
        self.rd = []


class Op:
    __slots__ = ("eng", "fn", "deps", "idx", "signal", "cnt", "dma", "dsem", "dtarget", "dprev")


class Prog:
    ENG = ("pe", "act", "dve", "pool", "sp")

    def __init__(self, nc, ndsem=12):
        self.nc = nc
        self.ops = {e: [] for e in self.ENG}
        self.ndsem = ndsem
        self.pending_bar = {}
        self.dma_since_bar = []

    def barrier(self):
        deps = []
        for e in self.ENG:
            for o in reversed(self.ops[e]):
                if not o.dma:
                    deps.append(o)
                    break
        deps.extend(self.dma_since_bar)
        self.dma_since_bar = []
        for e in self.ENG:
            self.pending_bar[e] = list(deps) + self.pending_bar.get(e, [])

    def op(self, eng, fn, reads=(), writes=(), dma=0, nobar=False):
        o = Op()
        o.eng = eng
        o.fn = fn
        o.dma = dma
        o.signal = False
        o.cnt = 0
        deps = set()
        for t in reads:
            if t.w is not None:
                deps.add(t.w)
        for t in writes:
            if t.w is not None:
                deps.add(t.w)
            deps.update(t.rc.values())
            deps.update(t.rd)
        for t in reads:
            if dma:
                t.rd.append(o)
            else:
                t.rc[eng] = o
        for t in writes:
            t.w = o
            t.rc = {}
            t.rd = []
        deps.discard(o)
        if eng == "pe" and not dma:
            deps = {d for d in deps if d.dma or d.eng != "pe"}
        if eng in self.pending_bar:
            deps.update(self.pending_bar.pop(eng))
        best = {}
        out = []
        for d in deps:
            if d.dma:
                out.append(d)
            else:
                b = best.get(d.eng)
                if b is None or d.idx > b.idx:
                    best[d.eng] = d
        out.extend(best.values())
        o.deps = out
        o.idx = len(self.ops[eng])
        self.ops[eng].append(o)
        if dma and not nobar:
            self.dma_since_bar.append(o)
        return o

    def emit(self):
        nc = self.nc
        for e in self.ENG:
            for o in self.ops[e]:
                for d in o.deps:
                    d.signal = True
        for e in self.ENG:
            c = 0
            for o in self.ops[e]:
                if not o.dma and o.signal:
                    c += 1
                o.cnt = c
        with contextlib.ExitStack() as st:
            esem = {e: st.enter_context(nc.semaphore("es_" + e)) for e in self.ENG}
            dsem = {e: [st.enter_context(nc.semaphore("ds_%s_%d" % (e, i))) for i in range(self.ndsem)]
                    for e in ("sp", "pool", "act")}
            for e in self.ENG:
                k = 0
                targets = [0] * self.ndsem
                for o in self.ops[e]:
                    if o.dma:
                        slot = k % self.ndsem
                        o.dsem = dsem[e][slot]
                        o.dprev = targets[slot]
                        targets[slot] += 16 * o.dma
                        o.dtarget = targets[slot]
                        k += 1
            block = st.enter_context(nc.Block())

            def run(eng_name):
                def body(e):
                    known = {}
                    for o in self.ops[eng_name]:
                        waits = {}
                        for d in o.deps:
                            if d.dma:
                                key, val = d.dsem, d.dtarget
                            else:
                                key, val = esem[d.eng], d.cnt
                            if waits.get(key, (None, 0))[1] < val:
                                waits[key] = (key, val)
                        if o.dma and o.dprev > 0:
                            if waits.get(o.dsem, (None, 0))[1] < o.dprev:
                                waits[o.dsem] = (o.dsem, o.dprev)
                        for key, val in waits.values():
                            if known.get(key, 0) < val:
                                e.wait_ge(key, val)
                                known[key] = val
                        if o.fn is None:
                            continue
                        rs_ = [getattr(e, nm)(*a, **kw) for (nm, a, kw) in o.fn]
                        if o.dma:
                            assert len(rs_) == o.dma
                            for ins in rs_:
                                ins.then_inc(o.dsem, 16)
                        elif o.signal:
                            rs_[-1].then_inc(esem[eng_name], 1)
                return body

            block.tensor(run("pe"))
            block.scalar(run("act"))
            block.vector(run("dve"))
            block.gpsimd(run("pool"))
            block.sync(run("sp"))


class Arena:
    def __init__(self, nc, nbytes):
        self.t = nc.alloc_sbuf_tensor("arena", [128, nbytes // 2], BF16)
        self.top = 0
        self.cap = nbytes
        self.peak = 0

    def alloc(self, free, dtype):
        n = int(np.prod(free))
        sz = n * (4 if dtype == F32 else 2)
        off = self.top
        self.top += (sz + 63) // 64 * 64
        self.peak = max(self.peak, self.top)
        assert self.top <= self.cap, "SBUF arena overflow %d > %d" % (self.top, self.cap)
        ap = self.t[:, off // 2: off // 2 + sz // 2]
        if dtype == F32:
            ap = ap.bitcast(F32)
        if len(free) == 2:
            ap = ap.rearrange("p (a b) -> p a b", a=free[0])
        elif len(free) == 3:
            ap = ap.rearrange("p (a b c) -> p a b c", a=free[0], b=free[1])
        return ap

    def mark(self):
        return self.top

    def release(self, m):
        self.top = m


def I(name, *a, **kw):
    return (name, a, kw)


def mkap(base, extra, dims):
    return AP(base.tensor, base.offset + extra, [list(base.ap[0])] + [list(d) for d in dims])


def make_consts():
    c = {}
    c["ident_bf"] = np.eye(128, dtype=np.float32).astype(ml_dtypes.bfloat16)
    c["ident_f"] = np.eye(128, dtype=np.float32)
    c["onesD"] = np.full((128, 128), 1.0 / D, dtype=np.float32).astype(ml_dtypes.bfloat16)
    c["onesH"] = np.full((128, 128), 1.0 / HD, dtype=np.float32).astype(ml_dtypes.bfloat16)
    inv_freq = (1.0 / (np.float32(10000.0) ** (np.arange(0, HD, 2, dtype=np.float32) / np.float32(HD)))).astype(np.float32)
    pos = np.arange(S, dtype=np.float32)
    ang = (pos[:, None] * inv_freq[None, :]).astype(np.float32)
    cos = np.cos(ang).astype(np.float32).reshape(NB, 128, 64).transpose(1, 0, 2)
    sin = np.sin(ang).astype(np.float32).reshape(NB, 128, 64).transpose(1, 0, 2)
    c["cos"] = np.ascontiguousarray(cos)
    c["sins"] = np.ascontiguousarray(np.stack([-sin, sin], axis=2))
    lg = np.log1p(-np.exp2(-5.0 - np.arange(NH, dtype=np.float64)))
    i = np.arange(128)
    ii, jj = i[None, :], i[:, None]
    ci, cj = ii // 64, jj // 64
    maskT = np.zeros((128, NH, 128), dtype=np.float64)
    for h in range(NH):
        same = np.exp(lg[h] * np.abs(ii - jj))
        prev = np.exp(lg[h] * (ii - jj).clip(min=0))
        maskT[:, h, :] = np.where(ci == cj, same, np.where(cj < ci, prev, 0.0))
    c["maskT"] = maskT.astype(np.float32)
    qd = np.exp(lg[:, None] * (i[None, :] + 1.0))
    c["qdec"] = np.ascontiguousarray(np.broadcast_to(qd[None], (128, NH, 128))).astype(np.float32)
    c["kdec"] = np.exp(lg[None, :] * (127.0 - i[:, None])).astype(np.float32)
    cd = [float(np.exp(lg[h] * 128.0)) for h in range(NH)]
    t = np.arange(128)
    c["mask01"] = (t[None, :] < t[:, None]).astype(np.float32)
    c["ones_col"] = np.ones((128, 1), dtype=np.float32)
    c["negmask"] = (np.float32(-30000.0) * (t[None, :] >= t[:, None]).astype(np.float32)).astype(ml_dtypes.bfloat16)
    c["mhalf"] = np.full((128, 1), -0.5, dtype=np.float32)
    c["cenM"] = (np.eye(128, dtype=np.float32) - np.float32(1.0 / HD)).astype(ml_dtypes.bfloat16)
    c["notmask"] = (t[None, :] >= t[:, None]).astype(np.float32)
    return c, cd


CONST_SPECS = [
    ("ident_bf", [128], BF16), ("ident_f", [128], F32), ("onesD", [128], BF16), ("onesH", [128], BF16),
    ("cos", [NB, 64], F32), ("sins", [NB, 2, 64], F32), ("maskT", [NH, 128], F32), ("qdec", [NH, 128], F32),
    ("kdec", [NH], F32), ("notmask", [128], F32), ("mask01", [128], F32), ("ones_col", [1], F32), ("cenM", [128], BF16), ("negmask", [128], BF16), ("mhalf", [1], F32),
]


def build_program(layers, first, final):
    nc = bass.Bass("TRN2", target_bir_lowering=False)
    _, CD = make_consts()

    def din(name, shape, dt=F32):
        return nc.dram_tensor(name, list(shape), dt, kind="ExternalInput").ap()

    x_d = din("x", [S, D])
    n1_d = din("norm1_gT", [128, DEPTH * 8])
    n2_d = din("norm2_gT", [128, DEPTH * 8])
    rg_d = din("ret_gT", [128, DEPTH * NH])
    sg_d = din("sb_gT", [128, DEPTH * NH])
    fg_d = din("final_gT", [128, 8])
    w_in_d = din("w_in", [DEPTH, D, INW])
    w_out_d = din("w_out", [DEPTH, D, D])
    w_gate_d = din("w_gate", [DEPTH, D, DFF])
    w_up_d = din("w_up", [DEPTH, D, DFF])
    w_down_d = din("w_down", [DEPTH, DFF, D])
    cdram = {}
    for name, free, dt in CONST_SPECS:
        cdram[name] = din("c_" + name, [128] + free, dt)
    y_d = nc.dram_tensor("y", [S, D], F32, kind="ExternalOutput").ap()

    A = Arena(nc, ARENA_BYTES)
    p = Prog(nc)
    ps = nc.alloc_psum_tensor("ps", [128, 8, 512], F32)
    Tps = [T("ps%d" % b) for b in range(8)]

    def psb(b):
        return ps[:, b, :]

    def psb16(b):
        return ps[:, b, :].bitcast(BF16)

    xT = A.alloc([8, S], F32)
    hT = A.alloc([8, S], BF16)
    Tx = [[T("x%d_%d" % (c, n)) for n in range(NT)] for c in range(8)]
    Th = [[T("h%d_%d" % (c, n)) for n in range(NT)] for c in range(8)]
    ring = [A.alloc([1024], BF16) for _ in range(RING)]
    Tring = [T("ring%d" % i) for i in range(RING)]
    C = {}
    TC = T("consts")
    for name, free, dt in CONST_SPECS:
        C[name] = A.alloc(free, dt)
    gvec = {}
    gsrc = (("n1", n1_d, DEPTH * 8), ("n2", n2_d, DEPTH * 8), ("rg", rg_d, DEPTH * NH), ("sg", sg_d, DEPTH * NH), ("fg", fg_d, 8))
    for name, src, n in gsrc:
        gvec[name] = A.alloc([n], F32)
    ins = [I("dma_start", out=C[name], in_=cdram[name]) for name, _, _ in CONST_SPECS]
    ins += [I("dma_start", out=gvec[name], in_=src) for name, src, _ in gsrc]
    p.op("sp", ins, writes=[TC], dma=len(ins))

    wq = []
    wstate = {"next_dma": 0, "next_use": 0}
    Txload = T("xload")

    def cn(src2d):
        return src2d.rearrange("(c p) n -> p c n", p=128)

    for l in layers:
        for h in range(NH):
            for j in range(4):
                c0 = j * 512 + h * 128
                wq.append(("cn", cn(w_in_d[l, :, c0:c0 + 128])))
        for hh in range(4):
            wq.append(("n", w_out_d[l, hh * 128:(hh + 1) * 128, :]))
        for h in range(NH):
            for j in range(3):
                c0 = 2048 + j * 512 + h * 128
                wq.append(("cn", cn(w_in_d[l, :, c0:c0 + 128])))
        for hh in range(4):
            wq.append(("n", w_out_d[l, 512 + hh * 128:512 + (hh + 1) * 128, :]))
        for g in range(NFC // FG):
            for f in range(g * FG, (g + 1) * FG):
                wq.append(("cn", cn(w_gate_d[l, :, f * 128:(f + 1) * 128])))
                wq.append(("cn", cn(w_up_d[l, :, f * 128:(f + 1) * 128])))
            for f in range(g * FG, (g + 1) * FG):
                wq.append(("n", w_down_d[l, f * 128:(f + 1) * 128, :]))

    def ring_view(slot, kind):
        ap = ring[slot]
        if kind == "cn":
            ap = ap.rearrange("p (c n) -> p c n", c=8)
        return ap

    def weight_done():
        k = wstate["next_dma"]
        if k >= len(wq):
            return
        kind, src = wq[k]
        slot = k % RING
        p.op("pool", [I("dma_start", out=ring_view(slot, kind), in_=src)], reads=([Txload] if k < RING else []), writes=[Tring[slot]], dma=1, nobar=True)
        wstate["next_dma"] = k + 1

    def next_weight(kind):
        k = wstate["next_use"]
        assert wq[k][0] == kind, (k, wq[k][0], kind)
        assert k < wstate["next_dma"]
        wstate["next_use"] = k + 1
        slot = k % RING
        return ring_view(slot, kind), Tring[slot]

    base_mark = A.mark()

    def phase_load_x():
        m = A.mark()
        xin = [A.alloc([D], F32) for _ in range(6)]
        Txin = [T("xin%d" % i) for i in range(6)]
        for b in range(NB):
            xi, txi = xin[b % 6], Txin[b % 6]
            p.op("sp", [I("dma_start", out=xi, in_=x_d[b * 128:(b + 1) * 128, :])], writes=[txi] + ([Txload] if b == NB - 3 else []), dma=1)
            for half in range(2):
                bank = (2 * b + half) % 8
                ins = [I("transpose", out=psb(bank)[:, j * 128:(j + 1) * 128],
                         in_=xi[:, (half * 4 + j) * 128:(half * 4 + j + 1) * 128], identity=C["ident_f"]) for j in range(4)]
                p.op("pe", ins, reads=[txi, TC], writes=[Tps[bank]])
                dst = xT[:, half * 4:half * 4 + 4, b * 128:(b + 1) * 128]
                src = psb(bank).rearrange("p (j t) -> p j t", j=4)
                wr = [Tx[c][b // 4] for c in range(half * 4, half * 4 + 4)]
                if half == 0:
                    p.op("act", [I("copy", out=dst, in_=src)], reads=[Tps[bank]], writes=wr)
                else:
                    p.op("dve", [I("tensor_copy", out=dst, in_=src)], reads=[Tps[bank]], writes=wr)
            if b == NB - 3:
                for _ in range(RING):
                    weight_done()
        p.barrier()
        A.release(m)

    def norm_scratch():
        sq = [A.alloc([TW], BF16) for _ in range(2)]
        rs = [A.alloc([TW], F32) for _ in range(2)]
        return sq, [T("sq0"), T("sq1")], rs, [T("rs0"), T("rs1")]

    def rmsnorm_tile(n, dst_fn, gcols, bank, scratch):
        sq, Tsq, rs, Trs = scratch
        tsl = slice(n * TW, (n + 1) * TW)
        for c in range(8):
            k = c % 2
            p.op("act", [I("activation", out=sq[k], in_=xT[:, c, tsl], func=AF.Square)], reads=[Tx[c][n]], writes=[Tsq[k]])
            p.op("pe", [I("matmul", psb(bank), lhsT=C["onesD"], rhs=sq[k], start=(c == 0), stop=(c == 7))],
                 reads=[Tsq[k], TC], writes=[Tps[bank]])
        r, tr_ = rs[n % 2], Trs[n % 2]
        p.op("act", [I("activation", out=r, in_=psb(bank), func=AF.Ln, bias=EPS)], reads=[Tps[bank]], writes=[tr_])
        p.op("act", [I("activation", out=r, in_=r, func=AF.Exp, scale=-0.5)], reads=[tr_], writes=[tr_])
        for c in range(8):
            d_, td_ = dst_fn(c)
            p.op("dve", [I("scalar_tensor_tensor", out=d_, in0=xT[:, c, tsl], scalar=gcols[:, c:c + 1], in1=r, op0=ALU.mult, op1=ALU.mult)],
                 reads=[Tx[c][n], tr_, TC], writes=[td_])

    def rmsnorm_h(gcols, barrier=True):
        m = A.mark()
        sc = norm_scratch()
        for n in range(NT):
            tsl = slice(n * TW, (n + 1) * TW)
            rmsnorm_tile(n, lambda c, n=n, tsl=tsl: (hT[:, c, tsl], Th[c][n]), gcols, n % 2, sc)
        if barrier:
            p.barrier()
            A.release(m)

    def wout_half(catT, Tcat):
        ws = [next_weight("n") for _ in range(4)]
        i = 0
        for n in range(NT):
            for c in range(8):
                bank = i % 4
                i += 1
                tsl = slice(n * TW, (n + 1) * TW)
                ins = [I("matmul", psb(bank), lhsT=ws[hh][0][:, c * 128:(c + 1) * 128], rhs=catT[:, hh, tsl],
                         start=(hh == 0), stop=(hh == 3)) for hh in range(4)]
                p.op("pe", ins, reads=[w[1] for w in ws] + [Tcat[hh][n] for hh in range(4)], writes=[Tps[bank]])
                p.op("dve", [I("tensor_tensor", out=xT[:, c, tsl], in0=psb(bank), in1=xT[:, c, tsl], op=ALU.add)],
                     reads=[Tps[bank], Tx[c][n]], writes=[Tx[c][n]])
        for _ in range(4):
            weight_done()

    def phase_retention(l, catT, Tcat):
        m = A.mark()
        qkT = A.alloc([2, S], BF16)
        qdT = A.alloc([S], BF16)
        ktok = A.alloc([NB, 128], BF16); v = A.alloc([NB, 128], BF16); vd = A.alloc([NB, 128], BF16)
        gateT = A.alloc([S], BF16)
        Sbf = A.alloc([NB, 128], BF16)
        Sf = A.alloc([128], F32)
        tA = [A.alloc([2, 128], F32) for _ in range(2)]
        tB = [A.alloc([2, 128], F32) for _ in range(2)]
        qrot = [A.alloc([128], BF16) for _ in range(2)]
        smT = [A.alloc([4, 128], BF16) for _ in range(2)]
        obf = [A.alloc([TW], BF16) for _ in range(2)]
        osq = [A.alloc([TW], BF16) for _ in range(2)]
        rr = [A.alloc([TW], F32) for _ in range(2)]
        Tqk, TqdT, Tktok, Tv, Tvd, Tgate, TSbf, TSf = [T(n) for n in "qk qdT ktok v vd gate Sbf Sf".split()]
        TtA = [T("tA0"), T("tA1")]; TtB = [T("tB0"), T("tB1")]; Tqrot = [T("qr0"), T("qr1")]
        TsmT = [T("sm0"), T("sm1")]
        Tobf = [T("ob0"), T("ob1")]; Tosq = [T("os0"), T("os1")]; Trr = [T("rr0"), T("rr1")]
        qT = qkT[:, 0, :]
        kT = qkT[:, 1, :]

        for h in range(NH):
            (wq_, Twq), (wk_, Twk), (wv_, Twv), (wg_, Twg) = [next_weight("cn") for _ in range(4)]

            def qkv(b):
                bank = 2 + (b % 2)
                bsl = slice(b * 128, (b + 1) * 128)
                ins = []
                if (wk_.offset - wq_.offset == 1024) and (wv_.offset - wk_.offset == 1024):
                    out3 = psb(bank)[:, 0:384].rearrange("p (j n) -> p j n", j=3)
                    for c in range(8):
                        ins.append(I("matmul", out3, lhsT=hT[:, c, bsl], rhs=mkap(wq_[:, c, :], 0, [[1024, 3], [1, 128]]),
                                     start=(c == 0), stop=(c == 7)))
                else:
                    for j, w_ in enumerate((wq_, wk_, wv_)):
                        for c in range(8):
                            ins.append(I("matmul", psb(bank)[:, j * 128:(j + 1) * 128], lhsT=hT[:, c, bsl], rhs=w_[:, c, :],
                                         start=(c == 0), stop=(c == 7)))
                p.op("pe", ins, reads=[Twq, Twk, Twv] + [Th[c][b // 4] for c in range(8)], writes=[Tps[bank]])

            def gate_tile(n):
                bank = n % 2
                tsl = slice(n * TW, (n + 1) * TW)
                ins = [I("matmul", psb(bank), lhsT=wg_[:, c, :], rhs=hT[:, c, tsl], start=(c == 0), stop=(c == 7)) for c in range(8)]
                p.op("pe", ins, reads=[Twg] + [Th[c][n] for c in range(8)], writes=[Tps[bank]])
                p.op("act", [I("activation", out=gateT[:, tsl], in_=psb(bank), func=AF.Silu)], reads=[Tps[bank]], writes=[Tgate])

            def evac(b):
                tbank = 4 + (b % 2)
                bsl = slice(b * 128, (b + 1) * 128)
                p.op("dve", [I("tensor_copy", out=qkT[:, :, bsl], in_=psb16(tbank)[:, 0:256].rearrange("p (j t) -> p j t", j=2))],
                     reads=[Tps[tbank]], writes=[Tqk])
                p.op("dve", [I("tensor_tensor", out=qdT[:, bsl], in0=psb16(tbank)[:, 0:128], in1=C["qdec"][:, h, :], op=ALU.mult)],
                     reads=[Tps[tbank], TC], writes=[TqdT])

            p.op("pool", [I("memset", Sf, 0.0)], writes=[TSf])
            p.op("pool", [I("memset", Sbf[:, 0, :], 0.0)], writes=[TSbf])
            qkv(0)
            for b in range(NB):
                if b + 1 < NB:
                    qkv(b + 1)
                bank = 2 + (b % 2)
                a_, ta_ = tA[b % 2], TtA[b % 2]
                b_, tb_ = tB[b % 2], TtB[b % 2]
                xq = psb(bank)[:, 0:256]
                x4 = mkap(xq, 0, [[128, 2], [64, 2], [1, 64]])
                xsw = mkap(xq, 64, [[128, 2], [-64, 2], [1, 64]])
                cos4 = mkap(C["cos"][:, b, :], 0, [[0, 2], [0, 2], [1, 64]])
                sin4 = mkap(C["sins"][:, b, :, :], 0, [[0, 2], [64, 2], [1, 64]])
                a4 = a_.rearrange("p q (a b) -> p q a b", a=2)
                b4 = b_.rearrange("p q (a b) -> p q a b", a=2)
                p.op("dve", [I("tensor_tensor", out=a4, in0=x4, in1=cos4, op=ALU.mult)], reads=[Tps[bank], TC], writes=[ta_])
                p.op("dve", [I("tensor_tensor", out=b4, in0=xsw, in1=sin4, op=ALU.mult)], reads=[Tps[bank], TC], writes=[tb_])
                qr, tqr = qrot[b % 2], Tqrot[b % 2]
                p.op("pool", [I("tensor_tensor", out=qr, in0=a_[:, 0, :], in1=b_[:, 0, :], op=ALU.add)], reads=[ta_, tb_], writes=[tqr])
                p.op("pool", [I("tensor_tensor", out=ktok[:, b, :], in0=a_[:, 1, :], in1=b_[:, 1, :], op=ALU.add)], reads=[ta_, tb_], writes=[Tktok])
                p.op("dve", [I("tensor_copy", out=v[:, b, :], in_=psb(bank)[:, 256:384])], reads=[Tps[bank]], writes=[Tv])
                p.op("dve", [I("tensor_scalar_mul", out=vd[:, b, :], in0=psb(bank)[:, 256:384], scalar1=C["kdec"][:, h:h + 1])],
                     reads=[Tps[bank], TC], writes=[Tvd])
                tbank = 4 + (b % 2)
                ins = [I("transpose", out=psb16(tbank)[:, 0:128], in_=qr, identity=C["ident_bf"]),
                       I("transpose", out=psb16(tbank)[:, 128:256], in_=ktok[:, b, :], identity=C["ident_bf"])]
                p.op("pe", ins, reads=[tqr, Tktok, TC], writes=[Tps[tbank]])
                if b >= 1:
                    evac(b - 1)
                    pbank = 6 + ((b - 1) % 2)
                    p.op("dve", [I("scalar_tensor_tensor", out=Sf, in0=Sf, scalar=CD[h], in1=psb(pbank)[:, 0:128], op0=ALU.mult, op1=ALU.add)],
                         reads=[TSf, Tps[pbank]], writes=[TSf])
                    p.op("pool", [I("tensor_copy", out=Sbf[:, b, :], in_=Sf)], reads=[TSf], writes=[TSbf])
                if b < NB - 1:
                    ubank = 6 + (b % 2)
                    p.op("pe", [I("matmul", psb(ubank)[:, 0:128], lhsT=ktok[:, b, :], rhs=vd[:, b, :], start=True, stop=True)],
                         reads=[Tktok, Tvd], writes=[Tps[ubank]])
                if b % 4 == 2:
                    gate_tile(b // 4)
            evac(NB - 1)
            for _ in range(4):
                weight_done()
            def tile_s1(n):
                sbank = n % 2
                obank = 2 + (n % 2)
                ins = []
                for q in range(4):
                    bsl = slice((4 * n + q) * 128, (4 * n + q + 1) * 128)
                    ins.append(I("matmul", psb(sbank)[:, q * 128:(q + 1) * 128], lhsT=kT[:, bsl], rhs=qT[:, bsl], start=True, stop=True))
                p.op("pe", ins, reads=[Tqk], writes=[Tps[sbank]])
                sm, tsm = smT[n % 2], TsmT[n % 2]
                mk3 = mkap(C["maskT"][:, h, :], 0, [[0, 4], [1, 128]])
                p.op("dve", [I("tensor_tensor", out=sm, in0=psb(sbank).rearrange("p (q i) -> p q i", q=4), in1=mk3, op=ALU.mult)],
                     reads=[Tps[sbank], TC], writes=[tsm])

            def tile_s1b(n):
                obank = 2 + (n % 2)
                sm, tsm = smT[n % 2], TsmT[n % 2]
                ins = []
                for q in range(4):
                    blk = 4 * n + q
                    bsl = slice(blk * 128, (blk + 1) * 128)
                    ins.append(I("matmul", psb(obank)[:, q * 128:(q + 1) * 128], lhsT=v[:, blk, :], rhs=sm[:, q, :], start=True, stop=False))
                    ins.append(I("matmul", psb(obank)[:, q * 128:(q + 1) * 128], lhsT=Sbf[:, blk, :], rhs=qdT[:, bsl], start=False, stop=True))
                p.op("pe", ins, reads=[Tv, tsm, TSbf, TqdT], writes=[Tps[obank]])
                p.op("act", [I("activation", out=obf[n % 2], in_=psb(obank), func=AF.Copy, scale=SCALE)], reads=[Tps[obank]], writes=[Tobf[n % 2]])

            def tile_s2(n):
                cbank = 4 + (n % 2)
                p.op("pe", [I("matmul", psb(cbank), lhsT=C["cenM"], rhs=obf[n % 2], start=True, stop=True)], reads=[Tobf[n % 2], TC], writes=[Tps[cbank]])
                p.op("act", [I("activation", out=osq[n % 2], in_=psb(cbank), func=AF.Square)], reads=[Tps[cbank]], writes=[Tosq[n % 2]])

            def tile_s3(n):
                tsl = slice(n * TW, (n + 1) * TW)
                cbank = 4 + (n % 2)
                vbank = 6 + (n % 2)
                r_, tr_ = rr[n % 2], Trr[n % 2]
                p.op("pe", [I("matmul", psb(vbank), lhsT=C["onesH"], rhs=osq[n % 2], start=True, stop=True)], reads=[Tosq[n % 2], TC], writes=[Tps[vbank]])
                p.op("act", [I("activation", out=r_, in_=psb(vbank), func=AF.Ln, bias=EPS)], reads=[Tps[vbank]], writes=[tr_])
                p.op("act", [I("activation", out=r_, in_=r_, func=AF.Exp, scale=-0.5)], reads=[tr_], writes=[tr_])
                p.op("pool", [I("tensor_tensor", out=r_, in0=r_, in1=gateT[:, tsl], op=ALU.mult)], reads=[tr_, Tgate], writes=[tr_])
                gcol = gvec["rg"][:, l * NH + h:l * NH + h + 1]
                p.op("dve", [I("scalar_tensor_tensor", out=catT[:, h, tsl], in0=psb(cbank), scalar=gcol, in1=r_, op0=ALU.mult, op1=ALU.mult)],
                     reads=[Tps[cbank], tr_, TC], writes=[Tcat[h][n]])

            for k in range(NT + 3):
                if k < NT:
                    tile_s1(k)
                if 0 <= k - 1 < NT:
                    tile_s1b(k - 1)
                if 0 <= k - 2 < NT:
                    tile_s2(k - 2)
                if 0 <= k - 3 < NT:
                    tile_s3(k - 3)
        p.barrier()
        A.release(m)

    def phase_sb(l, catT, Tcat):
        m = A.mark()
        NE, NW = 3, 3
        qTb = [A.alloc([S], BF16) for _ in range(2)]
        kTb = [A.alloc([S], BF16) for _ in range(2)]
        vb = [A.alloc([NB, 128], BF16) for _ in range(2)]
        TqT = [T("sqT0"), T("sqT1")]; TkT = [T("skT0"), T("skT1")]; Tv = [T("sv0"), T("sv1")]
        Gb = [A.alloc([520], F32) for _ in range(NE)]
        Qb = [A.alloc([520], F32) for _ in range(NE)]
        wb = [A.alloc([512], BF16) for _ in range(NW)]
        wTb = [A.alloc([4, 128], BF16) for _ in range(2)]
        carry = [A.alloc([1], F32) for _ in range(2)]
        osq = [A.alloc([TW], BF16) for _ in range(2)]
        rs = [A.alloc([TW], F32) for _ in range(2)]
        TG = [T("G%d" % i) for i in range(NE)]; TQ = [T("Q%d" % i) for i in range(NE)]
        for i in range(NE):
            p.op("pool", [I("memset", Gb[i][:, 512:513], 1.0)], writes=[TG[i]])
        Tw = [T("w%d" % i) for i in range(NW)]; TwT = [T("wT0"), T("wT1")]
        Tcarry = [T("cy0"), T("cy1")]; Tosq = [T("sos0"), T("sos1")]; Trs = [T("srs0"), T("srs1")]
        ZB = [0, 1, 2]
        PB = 3
        TB = [4, 5]
        OB = [6, 7]

        def proj_pieces(h):
            par = h % 2
            ws = [next_weight("cn") for _ in range(3)]
            (wq_, Twq), (wk_, Twk), (wv_, Twv) = ws
            pieces = []
            for j, (w_, tw_, dst, tdst) in enumerate(((wq_, Twq, qTb[par], TqT[par]), (wk_, Twk, kTb[par], TkT[par]))):
                for n in range(NT):
                    def piece(j=j, w_=w_, tw_=tw_, dst=dst, tdst=tdst, n=n):
                        tsl = slice(n * TW, (n + 1) * TW)
                        ins = [I("matmul", psb(PB), lhsT=w_[:, c, :], rhs=hT[:, c, tsl], start=(c == 0), stop=(c == 7)) for c in range(8)]
                        p.op("pe", ins, reads=[tw_] + [Th[c][n] for c in range(8)], writes=[Tps[PB]])
                        if j == 0:
                            p.op("act", [I("activation", out=dst[:, tsl], in_=psb(PB), func=AF.Copy, scale=SCALE)], reads=[Tps[PB]], writes=[tdst])
                        else:
                            p.op("act", [I("copy", out=dst[:, tsl], in_=psb(PB))], reads=[Tps[PB]], writes=[tdst])
                    pieces.append(piece)
            for g4 in range(4):
                def piece(g4=g4, wv_=wv_, Twv=Twv, par=par):
                    ins = []
                    for q in range(4):
                        bsl = slice((4 * g4 + q) * 128, (4 * g4 + q + 1) * 128)
                        for c in range(8):
                            ins.append(I("matmul", psb(PB)[:, q * 128:(q + 1) * 128], lhsT=hT[:, c, bsl], rhs=wv_[:, c, :], start=(c == 0), stop=(c == 7)))
                    p.op("pe", ins, reads=[Twv] + [Th[c][g4] for c in range(8)], writes=[Tps[PB]])
                    p.op("act", [I("copy", out=vb[par][:, 4 * g4:4 * g4 + 4, :], in_=psb(PB).rearrange("p (q e) -> p q e", q=4))],
                         reads=[Tps[PB]], writes=[Tv[par]])
                pieces.append(piece)

            def done():
                for _ in range(3):
                    weight_done()
            pieces.append(done)
            return pieces

        units = []
        head_start = []
        for h in range(NH):
            head_start.append(len(units))
            for Tq in range(NB):
                nk = 128 * (Tq + 1)
                hi = nk
                first = True
                while hi > 0:
                    lo = max(0, hi - 512)
                    units.append((h, Tq, lo, hi, first, lo == 0))
                    first = False
                    hi = lo
        U = len(units)

        def st_z(u):
            h, Tq, k0, k1, first, last = units[u]
            par = h % 2
            n_ = k1 - k0
            zb = ZB[u % 3]
            G, tG = Gb[u % NE], TG[u % NE]
            qsl = slice(Tq * 128, (Tq + 1) * 128)
            ins = [I("matmul", psb(zb)[:, 0:n_], lhsT=qTb[par][:, qsl], rhs=kTb[par][:, k0:k1], start=True, stop=not first)]
            if first:
                ins.append(I("matmul", psb(zb)[:, n_ - 128:n_], lhsT=C["ident_bf"], rhs=C["negmask"], start=False, stop=True))
            p.op("pe", ins, reads=[TqT[par], TkT[par], TC], writes=[Tps[zb]])
            p.op("act", [I("activation", out=G[:, 512 - n_:512], in_=psb(zb)[:, 0:n_], func=AF.Sigmoid, scale=-1.0)], reads=[Tps[zb]], writes=[tG])

        def st_scan(u):
            h, Tq, k0, k1, first, last = units[u]
            n_ = k1 - k0
            G, tG = Gb[u % NE], TG[u % NE]
            Q, tQ = Qb[u % NE], TQ[u % NE]
            gr = mkap(G, 512, [[-1, n_ + 1]])
            qr = mkap(Q, 512, [[-1, n_ + 1]])
            onr = mkap(C["ones_col"], 0, [[0, n_ + 1]])
            if first:
                p.op("dve", [I("tensor_tensor_scan", out=qr, data0=gr, data1=onr, initial=1.0, op0=ALU.mult, op1=ALU.mult)],
                     reads=[tG, TC], writes=[tQ])
            else:
                pu = u - 1
                pn = units[pu][3] - units[pu][2]
                cy = Qb[pu % NE][:, 512 - pn:513 - pn]
                p.op("dve", [I("tensor_tensor_scan", out=qr, data0=gr, data1=onr, initial=cy, op0=ALU.mult, op1=ALU.mult)],
                     reads=[tG, TC, TQ[pu % NE]], writes=[tQ])

        def st_w(u):
            h, Tq, k0, k1, first, last = units[u]
            n_ = k1 - k0
            Q, tQ = Qb[u % NE], TQ[u % NE]
            w_, tw_ = wb[u % NW], Tw[u % NW]
            p.op("pool", [I("tensor_tensor", out=w_[:, 0:n_], in0=Q[:, 513 - n_:513], in1=Q[:, 512 - n_:512], op=ALU.subtract)], reads=[tQ], writes=[tw_])

        def st_tr(u):
            h, Tq, k0, k1, first, last = units[u]
            nkb = (k1 - k0) // 128
            w_, tw_ = wb[u % NW], Tw[u % NW]
            wT, twT = wTb[u % 2], TwT[u % 2]
            tb = TB[u % 2]
            ins = [I("transpose", out=psb16(tb)[:, i * 128:(i + 1) * 128], in_=w_[:, i * 128:(i + 1) * 128], identity=C["ident_bf"]) for i in range(nkb)]
            p.op("pe", ins, reads=[tw_, TC], writes=[Tps[tb]])
            p.op("act", [I("copy", out=wT[:, 0:nkb, :], in_=psb16(tb)[:, 0:nkb * 128].rearrange("p (i t) -> p i t", i=nkb))],
                 reads=[Tps[tb]], writes=[twT])

        def st_av(u):
            h, Tq, k0, k1, first, last = units[u]
            par = h % 2
            nkb = (k1 - k0) // 128
            wT, twT = wTb[u % 2], TwT[u % 2]
            ob = OB[(Tq // 4) % 2]
            q = Tq % 4
            ins = [I("matmul", psb(ob)[:, q * 128:(q + 1) * 128], lhsT=vb[par][:, k0 // 128 + i, :], rhs=wT[:, i, :],
                     start=(first and i == 0), stop=(last and i == nkb - 1)) for i in range(nkb)]
            p.op("pe", ins, reads=[Tv[par], twT], writes=[Tps[ob]])
            if last and q == 3:
                n = Tq // 4
                tsl = slice(n * TW, (n + 1) * TW)
                oq, toq = osq[n % 2], Tosq[n % 2]
                r_, tr_ = rs[n % 2], Trs[n % 2]
                mbank = PB
                gcol = gvec["sg"][:, l * NH + h:l * NH + h + 1]

                def n1():
                    p.op("act", [I("activation", out=oq, in_=psb(ob), func=AF.Square)], reads=[Tps[ob]], writes=[toq])

                def n2():
                    p.op("pe", [I("matmul", psb(mbank), lhsT=C["onesH"], rhs=oq, start=True, stop=True)], reads=[toq, TC], writes=[Tps[mbank]])

                def n3():
                    p.op("act", [I("activation", out=r_, in_=psb(mbank), func=AF.Ln, bias=EPS)], reads=[Tps[mbank]], writes=[tr_])
                    p.op("act", [I("activation", out=r_, in_=r_, func=AF.Exp, scale=-0.5)], reads=[tr_], writes=[tr_])

                def n4():
                    p.op("dve", [I("scalar_tensor_tensor", out=catT[:, h, tsl], in0=psb(ob), scalar=gcol, in1=r_, op0=ALU.mult, op1=ALU.mult)],
                         reads=[Tps[ob], tr_, TC], writes=[Tcat[h][n]])
                deferred.setdefault(cur_it[0] + 1, []).append(n1)
                deferred.setdefault(cur_it[0] + 2, []).append(n2)
                deferred_end.setdefault(cur_it[0] + 2, []).append(n3)
                deferred.setdefault(cur_it[0] + 3, []).append(n4)

        for pc in proj_pieces(0):
            pc()
        pending = []
        deferred = {}
        deferred_end = {}
        cur_it = [0]
        for i in range(U + 10):
            cur_it[0] = i
            for f in deferred.pop(i, []):
                f()
            if 0 <= i - 4 < U:
                st_tr(i - 4)
            if 0 <= i - 5 < U:
                st_av(i - 5)
            if i < U:
                h = units[i][0]
                if i == head_start[h] and h + 1 < NH:
                    pending = proj_pieces(h + 1)
                st_z(i)
            if 0 <= i - 2 < U:
                st_w(i - 2)
            if 0 <= i - 1 < U:
                st_scan(i - 1)
            for f in deferred_end.pop(i, []):
                f()
            if pending and i < U and (i - head_start[units[i][0]]) % 3 == 2:
                pending.pop(0)()
        assert not pending and not deferred and not deferred_end
        p.barrier()
        A.release(m)

    def phase_ffn(l):
        m = A.mark()
        actT = A.alloc([FG, S], BF16)
        Tact = [[T("a%d_%d" % (f, n)) for n in range(NT)] for f in range(FG)]
        sg = [A.alloc([TW], BF16) for _ in range(2)]
        Tsg = [T("sg0"), T("sg1")]
        rmsnorm_h(gvec["n2"][:, l * 8:(l + 1) * 8], barrier=False)
        i = 0
        for g in range(NFC // FG):
            for fl in range(FG):
                (wg_, Twg), (wu_, Twu) = next_weight("cn"), next_weight("cn")
                for n in range(NT):
                    tsl = slice(n * TW, (n + 1) * TW)
                    gb = (2 * i) % 8
                    ub = (2 * i + 1) % 8
                    i += 1
                    for (w_, tw_, bank) in ((wg_, Twg, gb), (wu_, Twu, ub)):
                        ins = [I("matmul", psb(bank), lhsT=w_[:, c, :], rhs=hT[:, c, tsl], start=(c == 0), stop=(c == 7)) for c in range(8)]
                        p.op("pe", ins, reads=[tw_] + [Th[c][n] for c in range(8)], writes=[Tps[bank]])
                    s_, ts_ = sg[i % 2], Tsg[i % 2]
                    p.op("act", [I("activation", out=s_, in_=psb(gb), func=AF.Silu)], reads=[Tps[gb]], writes=[ts_])
                    p.op("dve", [I("tensor_tensor", out=actT[:, fl, tsl], in0=psb(ub), in1=s_, op=ALU.mult)], reads=[Tps[ub], ts_], writes=[Tact[fl][n]])
                weight_done()
                weight_done()
            wds = [next_weight("n") for _ in range(FG)]
            for c in range(8):
                for n in range(NT):
                    tsl = slice(n * TW, (n + 1) * TW)
                    bank = i % 8
                    i += 1
                    ins = [I("matmul", psb(bank), lhsT=wds[fl][0][:, c * 128:(c + 1) * 128], rhs=actT[:, fl, tsl], start=(fl == 0), stop=(fl == FG - 1))
                           for fl in range(FG)]
                    p.op("pe", ins, reads=[w[1] for w in wds] + [Tact[fl][n] for fl in range(FG)], writes=[Tps[bank]])
                    p.op("dve", [I("tensor_tensor", out=xT[:, c, tsl], in0=psb(bank), in1=xT[:, c, tsl], op=ALU.add)],
                         reads=[Tps[bank], Tx[c][n]], writes=[Tx[c][n]])
            for _ in range(FG):
                weight_done()
        p.barrier()
        A.release(m)

    def phase_out(do_norm):
        m = A.mark()
        sc = norm_scratch()
        yt = [A.alloc([8, TW], F32) for _ in range(2)]
        Tyt = [[T("yt%d_%d" % (k, c)) for c in range(8)] for k in range(2)]
        yo = [A.alloc([D], F32) for _ in range(4)]
        Tyo = [T("yo%d" % i) for i in range(4)]
        Tout = T("out")

        def norm(n):
            k = n % 2
            rmsnorm_tile(n, lambda c, k=k: (yt[k][:, c, :], Tyt[k][c]), gvec["fg"], n % 2, sc)

        def trans(n):
            for bq in range(4):
                b = 4 * n + bq
                y_, ty_ = yo[b % 4], Tyo[b % 4]
                for half in range(2):
                    bank = 2 + (2 * b + half) % 6
                    ins = []
                    rd = [TC]
                    for j in range(4):
                        c = half * 4 + j
                        if do_norm:
                            src = yt[n % 2][:, c, bq * 128:(bq + 1) * 128]
                            rd.append(Tyt[n % 2][c])
                        else:
                            src = xT[:, c, b * 128:(b + 1) * 128]
                            rd.append(Tx[c][n])
                        ins.append(I("transpose", out=psb(bank)[:, j * 128:(j + 1) * 128], in_=src, identity=C["ident_f"]))
                    p.op("pe", ins, reads=rd, writes=[Tps[bank]])
                    if half == 0:
                        p.op("act", [I("copy", out=y_[:, 0:512], in_=psb(bank))], reads=[Tps[bank]], writes=[ty_])
                    else:
                        p.op("dve", [I("tensor_copy", out=y_[:, 512:1024], in_=psb(bank))], reads=[Tps[bank]], writes=[ty_])
                p.op("sp", [I("dma_start", out=y_d[b * 128:(b + 1) * 128, :], in_=y_)], reads=[ty_], writes=[Tout], dma=1)

        if do_norm:
            norm(0)
        for n in range(NT):
            if do_norm and n + 1 < NT:
                norm(n + 1)
            trans(n)
        p.op("sp", None, reads=[Tout])
        A.release(m)

    import os
    KSTOP = os.environ.get("KSTOP", "")
    phase_load_x()
    stopped = False
    for l in layers:
        m = A.mark()
        catT = A.alloc([4, S], BF16)
        Tcat = [[T("cat%d_%d" % (hh, n)) for n in range(NT)] for hh in range(4)]
        rmsnorm_h(gvec["n1"][:, l * 8:(l + 1) * 8])
        if KSTOP == "norm1":
            stopped = True
            break
        phase_retention(l, catT, Tcat)
        if KSTOP == "ret":
            stopped = True
            break
        wout_half(catT, Tcat)
        if KSTOP == "wout1":
            stopped = True
            break
        phase_sb(l, catT, Tcat)
        if KSTOP == "sb":
            stopped = True
            break
        wout_half(catT, Tcat)
        if KSTOP == "wout2":
            p.barrier()
            stopped = True
            break
        A.release(m)
        phase_ffn(l)
    if stopped:
        A.release(base_mark)
    phase_out(final and not stopped)
    if not stopped:
        assert wstate["next_use"] == len(wq), (wstate, len(wq))
    print("SBUF arena peak bytes/partition:", A.peak, "ops:", {e: len(p.ops[e]) for e in p.ENG})
    p.emit()
    return nc


_PROG_CACHE = {}


def _get_prog(key):
    if key not in _PROG_CACHE:
        _PROG_CACHE[key] = build_program(*key)
    return _PROG_CACHE[key]


def _colsT(g, n):
    g = np.asarray(g, dtype=np.float32).reshape(-1, n, 128)
    return np.ascontiguousarray(g.transpose(2, 0, 1).reshape(128, -1))


FUSED = True


def kernel(x, norm1_g, w_in, ret_norm_g, sb_norm_g, w_out, norm2_g, w_gate, w_up, w_down, final_g):
    consts, _ = make_consts()
    shared = {
        "norm1_gT": _colsT(norm1_g, 8), "norm2_gT": _colsT(norm2_g, 8),
        "ret_gT": _colsT(ret_norm_g, NH), "sb_gT": _colsT(sb_norm_g, NH),
        "final_gT": _colsT(final_g, 8),
        "w_in": np.ascontiguousarray(w_in, dtype=np.float32), "w_out": np.ascontiguousarray(w_out, dtype=np.float32),
        "w_gate": np.ascontiguousarray(w_gate, dtype=np.float32), "w_up": np.ascontiguousarray(w_up, dtype=np.float32),
        "w_down": np.ascontiguousarray(w_down, dtype=np.float32),
    }
    for name, _, _ in CONST_SPECS:
        shared["c_" + name] = consts[name]
    xs = [np.ascontiguousarray(x[i], dtype=np.float32) for i in range(NCORES)]
    if FUSED:
        stages = [((0, 1), True, True)]
    else:
        stages = [((0,), True, False), ((1,), True, True)]
    for (layers, first, final) in stages:
        nc = _get_prog((tuple(layers), first, final))
        in_maps = [dict(shared, x=xs[i]) for i in range(NCORES)]
        res = run_bass_kernel_spmd(nc, in_maps, core_ids=list(range(NCORES)))
        xs = [np.asarray(res.results[i]["y"], dtype=np.float32) for i in range(NCORES)]
    return np.stack(xs, axis=0).astype(np.float32)
```
